# Optimizing a Trainium2 kernel written in Bass

```python
import jax, jax.numpy as jnp
from jax import lax
import numpy as np

D_MODEL = 1024
BATCH = 8
SEQ = 8192
DEPTH = 2

GLA_HEADS = 4
GLA_DK = 32
GLA_DV = 64
GLA_RANK = 16
GLA_TAU = 16.0
GLA_CHUNK = 64
SWA_HEADS = 8
SWA_KV_HEADS = 2
SWA_HD = 64
WINDOW = 128
BLOCK = 128
N_BUCKETS = 32
MAX_DISTANCE = 128
CONV_CH = 256
CONV_WIDTH = 31
D_FF = 2816
FFN_CONV_WIDTH = 3
EPS = 1e-6

GLA_QK = GLA_HEADS * GLA_DK
GLA_V = GLA_HEADS * GLA_DV
SWA_Q = SWA_HEADS * SWA_HD
SWA_KV = SWA_KV_HEADS * SWA_HD
D_MIX = GLA_V + SWA_Q + CONV_CH
IN_SPLITS = (GLA_QK, GLA_QK, GLA_V, GLA_RANK, GLA_V, SWA_Q, SWA_KV, SWA_KV, CONV_CH, CONV_CH)
D_IN = GLA_QK + GLA_QK + GLA_V + GLA_RANK + GLA_V + SWA_Q + SWA_KV + SWA_KV + CONV_CH + CONV_CH

kernel_name = "hymba_gla_swa_conformer_convffn"


def _rms(x, g):
    x32 = x.astype(jnp.float32)
    y = x32 * lax.rsqrt(jnp.mean(x32 * x32, axis=-1, keepdims=True) + EPS)
    return (y * g.astype(jnp.float32)).astype(x.dtype)


def _causal_dwconv(x, w, b):
    width, ch = w.shape
    y = lax.conv_general_dilated(x, w[:, None, :].astype(x.dtype), window_strides=(1,),
                                 padding=[(width - 1, 0)],
                                 dimension_numbers=('NWC', 'WIO', 'NWC'),
                                 feature_group_count=ch)
    return y + b.astype(x.dtype)


def _t5_bucket(dist):
    max_exact = N_BUCKETS // 2
    d = jnp.maximum(dist, 0)
    d_f = jnp.maximum(d, 1).astype(jnp.float32)
    large = max_exact + (jnp.log(d_f / max_exact) / np.float32(np.log(MAX_DISTANCE / max_exact))
                         * (N_BUCKETS - max_exact)).astype(jnp.int32)
    large = jnp.minimum(large, N_BUCKETS - 1)
    return jnp.where(d < max_exact, d, large)


def _band_bias_and_mask(rel_bias):
    i = jnp.arange(BLOCK)[:, None]
    j = jnp.arange(2 * BLOCK)[None, :]
    dist = BLOCK + i - j
    band = (dist >= 0) & (dist < WINDOW)
    bias = rel_bias.astype(jnp.float32)[_t5_bucket(dist)]
    return jnp.transpose(bias, (2, 0, 1)), band


def _gla(q, k, v, z, r, w_a2, b_a, out_g):
    B, S = q.shape[:2]
    H, C = GLA_HEADS, GLA_CHUNK
    N = S // C
    f32 = jnp.float32
    logit = (z @ w_a2 + b_a).astype(f32)
    log_a = jax.nn.log_sigmoid(logit) / GLA_TAU

    def chunks(t, d):
        return t.astype(f32).reshape(B, N, C, H, d).transpose(0, 3, 1, 2, 4)

    qc = chunks(q, GLA_DK) * (GLA_DK ** -0.5)
    kc = chunks(k, GLA_DK)
    vc = chunks(v, GLA_DV)
    bcum = jnp.cumsum(chunks(log_a, GLA_DK), axis=3)
    b_last = bcum[:, :, :, -1:, :]
    q_dec = qc * jnp.exp(bcum)
    k_inv = kc * jnp.exp(-bcum)
    k_tail = kc * jnp.exp(b_last - bcum)
    causal = jnp.tril(jnp.ones((C, C), dtype=bool))
    attn = jnp.where(causal, jnp.einsum('bhnid,bhnjd->bhnij', q_dec, k_inv), 0.0)
    o_intra = jnp.einsum('bhnij,bhnjv->bhniv', attn, vc)
    dS = jnp.einsum('bhnjd,bhnjv->bhndv', k_tail, vc)
    decay = jnp.exp(b_last[:, :, :, 0, :])

    def step(s_prev, inp):
        dec, ds = inp
        return dec[..., None] * s_prev + ds, s_prev

    s0 = jnp.zeros((B, H, GLA_DK, GLA_DV), f32)
    _, s_before = lax.scan(step, s0, (jnp.moveaxis(decay, 2, 0), jnp.moveaxis(dS, 2, 0)))
    s_before = jnp.moveaxis(s_before, 0, 2)
    o = o_intra + jnp.einsum('bhnid,bhndv->bhniv', q_dec, s_before)
    o = o.transpose(0, 2, 3, 1, 4).reshape(B, S, H, GLA_DV)
    o = o * lax.rsqrt(jnp.mean(o * o, axis=-1, keepdims=True) + EPS)
    o = o.reshape(B, S, GLA_V) * out_g.astype(f32)
    o = o * jax.nn.silu(r.astype(f32))
    return o.astype(q.dtype)


def _swa(q, k, v, q_g, k_g, sinks, band_bias, band):
    B, S = q.shape[:2]
    nb = S // BLOCK
    G = SWA_HEADS // SWA_KV_HEADS
    q = _rms(q.reshape(B, S, SWA_HEADS, SWA_HD), q_g)
    k = _rms(k.reshape(B, S, SWA_KV_HEADS, SWA_HD), k_g)
    v = v.reshape(B, S, SWA_KV_HEADS, SWA_HD)
    qb = q.reshape(B, nb, BLOCK, SWA_KV_HEADS, G, SWA_HD)
    pad = ((0, 0), (BLOCK, 0), (0, 0), (0, 0))
    kb = jnp.pad(k, pad).reshape(B, nb + 1, BLOCK, SWA_KV_HEADS, SWA_HD)
    vb = jnp.pad(v, pad).reshape(B, nb + 1, BLOCK, SWA_KV_HEADS, SWA_HD)
    kwin = jnp.concatenate([kb[:, :-1], kb[:, 1:]], axis=2)
    vwin = jnp.concatenate([vb[:, :-1], vb[:, 1:]], axis=2)
    scores = jnp.einsum('bnqhgd,bnkhd->bnhgqk', qb, kwin,
                        preferred_element_type=jnp.float32) * (SWA_HD ** -0.5)
    scores = scores + band_bias.reshape(SWA_KV_HEADS, G, BLOCK, 2 * BLOCK)
    key_pos = (jnp.arange(nb)[:, None, None] * BLOCK - BLOCK + jnp.arange(2 * BLOCK)[None, None, :])
    mask = band[None] & (key_pos >= 0)
    scores = jnp.where(mask[None, :, None, None], scores, -1e30)
    sink = sinks.astype(jnp.float32).reshape(1, 1, SWA_KV_HEADS, G, 1, 1)
    m = jnp.maximum(jnp.max(scores, axis=-1, keepdims=True), sink)
    p = jnp.exp(scores - m)
    probs = p / (jnp.sum(p, axis=-1, keepdims=True) + jnp.exp(sink - m))
    out = jnp.einsum('bnhgqk,bnkhd->bnqhgd', probs.astype(v.dtype), vwin)
    return out.reshape(B, S, SWA_Q)


def _conformer_conv(a, gate, dw_w, dw_b, ln_g, ln_b):
    u = a * jax.nn.sigmoid(gate)
    y = _causal_dwconv(u, dw_w, dw_b).astype(jnp.float32)
    mu = jnp.mean(y, axis=-1, keepdims=True)
    var = jnp.mean(jnp.square(y - mu), axis=-1, keepdims=True)
    y = (y - mu) * lax.rsqrt(var + EPS) * ln_g.astype(jnp.float32) + ln_b.astype(jnp.float32)
    return jax.nn.silu(y).astype(a.dtype)


def setup_inputs(seed: int = 0) -> dict:
    key = jax.random.key(seed)
    ks = jax.random.split(key, 24)
    f32 = jnp.float32

    def nrm(k, shape, scale):
        return jax.random.normal(k, shape, f32) * scale

    L = DEPTH
    return {
        "x": nrm(ks[0], (BATCH, SEQ, D_MODEL), 1.0),
        "attn_norm_g": 1.0 + nrm(ks[1], (L, D_MODEL), 0.02),
        "w_in": nrm(ks[2], (L, D_MODEL, D_IN), D_MODEL ** -0.5),
        "gla_w_a2": nrm(ks[3], (L, GLA_RANK, GLA_QK), GLA_RANK ** -0.5),
        "gla_b_a": nrm(ks[4], (L, GLA_QK), 0.1),
        "gla_out_g": 1.0 + nrm(ks[5], (L, GLA_V), 0.02),
        "swa_q_g": 1.0 + nrm(ks[6], (L, SWA_HD), 0.02),
        "swa_k_g": 1.0 + nrm(ks[7], (L, SWA_HD), 0.02),
        "swa_sinks": nrm(ks[8], (L, SWA_HEADS), 0.5),
        "rel_bias": nrm(ks[9], (N_BUCKETS, SWA_HEADS), 0.5),
        "conv_dw_w": nrm(ks[10], (L, CONV_WIDTH, CONV_CH), CONV_WIDTH ** -0.5),
        "conv_dw_b": nrm(ks[11], (L, CONV_CH), 0.02),
        "conv_ln_g": 1.0 + nrm(ks[12], (L, CONV_CH), 0.02),
        "conv_ln_b": nrm(ks[13], (L, CONV_CH), 0.02),
        "branch_scale": 1.0 + nrm(ks[14], (L, D_MIX), 0.02),
        "w_out": nrm(ks[15], (L, D_MIX, D_MODEL), (2.0 * DEPTH * D_MIX) ** -0.5),
        "ffn_norm_g": 1.0 + nrm(ks[16], (L, D_MODEL), 0.02),
        "w_up": nrm(ks[17], (L, D_MODEL, 2 * D_FF), D_MODEL ** -0.5),
        "ffn_conv_w": nrm(ks[18], (L, FFN_CONV_WIDTH, 2 * D_FF), FFN_CONV_WIDTH ** -0.5),
        "ffn_conv_b": nrm(ks[19], (L, 2 * D_FF), 0.02),
        "w_down": nrm(ks[20], (L, D_FF, D_MODEL), (2.0 * DEPTH * D_FF) ** -0.5),
    }


def reference(x, attn_norm_g, w_in, gla_w_a2, gla_b_a, gla_out_g, swa_q_g, swa_k_g, swa_sinks,
              rel_bias, conv_dw_w, conv_dw_b, conv_ln_g, conv_ln_b, branch_scale, w_out,
              ffn_norm_g, w_up, ffn_conv_w, ffn_conv_b, w_down):
    band_bias, band = _band_bias_and_mask(rel_bias)
    split_idx = list(np.cumsum(IN_SPLITS)[:-1])
    for l in range(DEPTH):
        h = _rms(x, attn_norm_g[l])
        proj = h @ w_in[l]
        (q_a, k_a, v_a, z_a, r_a, q_b, k_b, v_b, c_a, c_gate) = jnp.split(proj, split_idx, axis=-1)
        o_a = _gla(q_a, k_a, v_a, z_a, r_a, gla_w_a2[l], gla_b_a[l], gla_out_g[l])
        o_b = _swa(q_b, k_b, v_b, swa_q_g[l], swa_k_g[l], swa_sinks[l], band_bias, band)
        o_c = _conformer_conv(c_a, c_gate, conv_dw_w[l], conv_dw_b[l], conv_ln_g[l], conv_ln_b[l])
        mix = jnp.concatenate([o_a, o_b, o_c], axis=-1) * branch_scale[l]
        x = x + mix @ w_out[l]
        h = _rms(x, ffn_norm_g[l])
        u = _causal_dwconv(h @ w_up[l], ffn_conv_w[l], ffn_conv_b[l])
        gate, val = jnp.split(u, 2, axis=-1)
        x = x + (jax.nn.silu(gate) * val) @ w_down[l]
    return x
```

```python
import numpy as np
import concourse.bass as bass
import concourse.mybir as mybir
from concourse.bass_utils import run_bass_kernel_spmd
from contextlib import ExitStack

F32 = mybir.dt.float32
BF16 = mybir.dt.bfloat16
AF = mybir.ActivationFunctionType
ALU = mybir.AluOpType

ENGS = ("pe", "act", "dve", "pool", "sp")
SEM_LIMIT = 6000

D = 1024
T = 512
NB = T // 128
DIN = 2064
DFF = 2816
NJ = DFF // 128
EPS = 1e-6
NPAR = 281
WCOLS = 2304
import os as _os
_DBG_STAGE = int(_os.environ.get("KSTAGE", "99"))


class Tile:
    _n = 0

    def __init__(self, t, nsub=1, name="", excl=False):
        self.t = t
        self.nsub = nsub
        self.name = name
        self.excl = excl
        Tile._n += 1
        self.id = Tile._n
        self.dma_sem = None
        self.dma_cnt = 0

    def __getitem__(self, k):
        return self.t[k]

    def all(self):
        return [(self.id, i) for i in range(self.nsub)]

    def s(self, i, n=1):
        return [(self.id, j) for j in range(i, i + n)]


def regs(x):
    out = []
    if x is None:
        return out
    if isinstance(x, Tile):
        return x.all()
    if isinstance(x, tuple) and len(x) == 2 and isinstance(x[0], int):
        return [x]
    for e in x:
        out.extend(regs(e))
    return out


class _RecIns:
    def __init__(self, c):
        self.c = c

    def then_inc(self, sem, v):
        self.c[3] = (sem, v)
        return self


class _Rec:
    def __init__(self):
        self.calls = []

    def __getattr__(self, name):
        def f(*a, **kw):
            c = [name, a, kw, None]
            self.calls.append(c)
            return _RecIns(c)
        return f


class Op:
    __slots__ = ("eng", "calls", "deps", "has_dep", "tok", "dma_tile", "ndma", "dma_val")

    def __init__(self, eng, fn):
        self.eng = eng
        self.calls = None
        self.deps = []
        self.has_dep = False
        self.tok = None
        self.dma_tile = None
        self.ndma = 0
        self.dma_val = 0


class KB:
    def __init__(self, nc):
        self.nc = nc
        self.es = ExitStack()
        self.ops = {e: [] for e in ENGS}
        self.last_w = {}
        self.readers = {}
        self.nsem = 0
        self.sb_bytes = 0
        self.excl_ids = set()

    def sb(self, name, shape, dtype, nsub=1):
        t = self.es.enter_context(self.nc.sbuf_tensor(name, list(shape), dtype))
        n = 1
        for d in shape[1:]:
            n *= d
        self.sb_bytes += n * (2 if dtype == BF16 else 4)
        return Tile(t, nsub, name)

    def ps(self, name, shape, dtype):
        t = self.es.enter_context(self.nc.psum_tensor(name, list(shape), dtype))
        tl = Tile(t, 1, name, excl=True)
        self.excl_ids.add(tl.id)
        return tl

    def sem(self, name):
        self.nsem += 1
        return self.es.enter_context(self.nc.semaphore(name))

    def op(self, eng, fn, r=None, w=None, dma_tile=None, ndma=0):
        o = Op(eng, fn)
        rr = regs(r)
        ww = regs(w)
        ex = [q for q in rr if q[0] in self.excl_ids and q not in ww]
        if ex:
            ww = ww + ex
        deps = []
        for k in rr:
            lw = self.last_w.get(k)
            if lw is not None:
                deps.append(lw)
        for k in ww:
            lw = self.last_w.get(k)
            if lw is not None:
                deps.append(lw)
            deps.extend(self.readers.get(k, ()))
        seen = set()
        for d in deps:
            if id(d) in seen:
                continue
            seen.add(id(d))
            if d.eng == eng and d.ndma == 0 and eng in ("pe", "sp"):
                continue
            o.deps.append(d)
            d.has_dep = True
        for k in rr:
            self.readers.setdefault(k, []).append(o)
        for k in ww:
            self.last_w[k] = o
            self.readers[k] = []
        if ndma:
            o.ndma = ndma
            o.dma_tile = dma_tile
            if dma_tile.dma_sem is None:
                dma_tile.dma_sem = self.sem("d%d_%s" % (dma_tile.id, dma_tile.name))
            dma_tile.dma_cnt += 16 * ndma
            o.dma_val = dma_tile.dma_cnt
            rec = _Rec()
            fn(rec, dma_tile.dma_sem)
            assert len(rec.calls) == ndma
        else:
            rec = _Rec()
            fn(rec)
            assert len(rec.calls) == 1
        o.calls = rec.calls
        self.ops[eng].append(o)
        return o

    def pe(self, fn, r=None, w=None):
        return self.op("pe", fn, r, w)

    def act(self, fn, r=None, w=None):
        return self.op("act", fn, r, w)

    def dve(self, fn, r=None, w=None):
        return self.op("dve", fn, r, w)

    def pool(self, fn, r=None, w=None):
        return self.op("pool", fn, r, w)

    def dma(self, fn, tile, r=None, w=None, n=1, eng="sp"):
        return self.op(eng, fn, r, w, dma_tile=tile, ndma=n)

    def finalize(self):
        for e in ENGS:
            cnt = 0
            cur = None
            for o in self.ops[e]:
                if o.ndma:
                    o.tok = (o.dma_tile.dma_sem, o.dma_val)
                    continue
                if not o.has_dep:
                    continue
                if cur is None or cnt >= SEM_LIMIT:
                    cur = self.sem("e_%s_%d" % (e, self.nsem))
                    cnt = 0
                cnt += 1
                o.tok = (cur, cnt)
        stats = {e: [0, 0] for e in ENGS}
        ops = self.ops

        def run(e, engine):
            known = {}
            for o in ops[e]:
                need = {}
                for d in o.deps:
                    s, v = d.tok
                    if known.get(id(s), 0) >= v:
                        continue
                    if need.get(id(s), (None, 0))[1] < v:
                        need[id(s)] = (s, v)
                for s, v in need.values():
                    engine.wait_ge(s, v)
                    known[id(s)] = v
                    stats[e][1] += 1
                if o.ndma:
                    for name, a, kw, inc in o.calls:
                        getattr(engine, name)(*a, **kw).then_inc(inc[0], inc[1])
                else:
                    name, a, kw, _ = o.calls[0]
                    ins = getattr(engine, name)(*a, **kw)
                    if o.tok is not None:
                        ins.then_inc(o.tok[0], 1)
                stats[e][0] += 1

        with self.nc.Block() as block:
            @block.tensor
            def _(eng):
                run("pe", eng)

            @block.scalar
            def _(eng):
                run("act", eng)

            @block.vector
            def _(eng):
                run("dve", eng)

            @block.gpsimd
            def _(eng):
                run("pool", eng)

            @block.sync
            def _(eng):
                run("sp", eng)
        self.stats = stats
        self.es.close()


def build(S, L=2, dbg=False):
    NT = S // T
    nc = bass.Bass("TRN2", target_bir_lowering=False)
    x_d = nc.dram_tensor("x", [128, 8, S], F32, kind="ExternalInput").ap()
    win_d = nc.dram_tensor("w_in", [L, D, DIN], F32, kind="ExternalInput").ap()
    wout_d = nc.dram_tensor("w_out", [L, D, D], F32, kind="ExternalInput").ap()
    wup_d = nc.dram_tensor("w_up", [L, D, 2 * DFF], F32, kind="ExternalInput").ap()
    wdn_d = nc.dram_tensor("w_down", [L, DFF, D], F32, kind="ExternalInput").ap()
    wa2_d = nc.dram_tensor("wa2", [L, 16, 128], F32, kind="ExternalInput").ap()
    par_d = nc.dram_tensor("params", [L, 128, NPAR], F32, kind="ExternalInput").ap()
    bias_d = nc.dram_tensor("biasT", [128, 8 * 2 * 128], F32, kind="ExternalInput").ap()
    mask_d = nc.dram_tensor("maskT", [128, 2 * 128], F32, kind="ExternalInput").ap()
    out_d = nc.dram_tensor("out", [128, 8, S], F32, kind="ExternalOutput").ap()
    if dbg:
        dbg_d = nc.dram_tensor("dbg", [128, 8 * T], BF16, kind="ExternalOutput").ap()
        Ddbg = Tile(None, 1, "dbg")
    wins_d = nc.dram_tensor("wins", [L, D, WCOLS], BF16, kind="Internal").ap()
    wouts_d = nc.dram_tensor("wouts", [L, D, D], BF16, kind="Internal").ap()
    wups_d = nc.dram_tensor("wups", [L, 11, 128, 8, 512], BF16, kind="Internal").ap()
    wdns_d = nc.dram_tensor("wdns", [L, 8, 128, NJ, 128], BF16, kind="Internal").ap()

    k = KB(nc)
    Dwin = [Tile(None, 11, "wins%d" % l) for l in range(L)]
    Dwout = [Tile(None, 2, "wouts%d" % l) for l in range(L)]
    Dwup = [Tile(None, 11, "wups%d" % l) for l in range(L)]
    Dwdn = [Tile(None, 8, "wdns%d" % l) for l in range(L)]
    Dout = Tile(None, NT * 8, "out")
    xst = [Tile(None, 1, "xst%d" % c) for c in range(8)]
    xo_t = [Tile(None, 1, "xo%d" % c) for c in range(8)]

    xT = k.sb("xT", [128, 8, T], F32, nsub=8)
    ident_f = k.sb("ident_f", [128, 128], F32)
    ident_b = k.sb("ident_b", [128, 128], BF16)
    ones_f = k.sb("ones_f", [128, 128], F32)
    ones_b = k.sb("ones_b", [128, 128], BF16)
    blk64 = k.sb("blk64", [128, 128], BF16)
    causalT = k.sb("causalT", [128, 128], F32)
    scanm = k.sb("scanm", [128, T], F32)
    EBt = k.sb("EBt", [128, 8 * 2 * 128], BF16)
    maskt = k.sb("maskt", [128, 2 * 128], F32)
    eps_t = k.sb("eps_t", [128, 1], F32)
    one_t = k.sb("one_t", [128, 1], F32)
    par = [k.sb("par%d" % l, [128, NPAR], F32) for l in range(L)]
    der = [k.sb("der%d" % l, [128, 16], F32) for l in range(L)]
    wa2f = [k.sb("wa2f%d" % l, [16, 128], F32) for l in range(L)]
    wa2b = [k.sb("wa2b%d" % l, [128, 128], BF16) for l in range(L)]

    k.pool(lambda e: e.memset(ones_f[:], 1.0), w=ones_f)
    k.pool(lambda e: e.memset(eps_t[:], EPS), w=eps_t)
    k.pool(lambda e: e.memset(one_t[:], 1.0), w=one_t)
    k.pool(lambda e: e.affine_select(out=ident_f[:], in_=ones_f[:], pattern=[[-1, 128]], compare_op=ALU.is_equal,
                                     fill=0.0, base=0, channel_multiplier=1), r=ones_f, w=ident_f)
    k.pool(lambda e: e.affine_select(out=causalT[:], in_=ones_f[:], pattern=[[1, 128]], compare_op=ALU.is_ge,
                                     fill=0.0, base=0, channel_multiplier=-1), r=ones_f, w=causalT)
    k.dve(lambda e: e.tensor_copy(out=ident_b[:], in_=ident_f[:]), r=ident_f, w=ident_b)
    k.dve(lambda e: e.tensor_copy(out=ones_b[:], in_=ones_f[:]), r=ones_f, w=ones_b)
    k.pool(lambda e: e.memset(blk64[:], 0.0), w=blk64)
    k.pool(lambda e: e.memset(blk64[0:64, 0:64], 1.0), w=blk64)
    k.pool(lambda e: e.memset(blk64[64:128, 64:128], 1.0), w=blk64)
    hm4 = k.sb("hm4", [128, 4], F32)
    hm2 = k.sb("hm2", [128, 2], F32)
    for hm_, w_, n_ in ((hm4, 32, 4), (hm2, 64, 2)):
        k.pool(lambda e, hm_=hm_, w_=w_, n_=n_: e.affine_select(out=hm_[:], in_=ones_f[:, 0:n_], pattern=[[-w_, n_]], compare_op=ALU.is_ge,
                                                            fill=0.0, base=0, channel_multiplier=1), r=ones_f, w=hm_)
        k.pool(lambda e, hm_=hm_, w_=w_, n_=n_: e.affine_select(out=hm_[:], in_=hm_[:], pattern=[[w_, n_]], compare_op=ALU.is_ge,
                                                            fill=0.0, base=w_ - 1, channel_multiplier=-1), r=hm_, w=hm_)
    k.pool(lambda e: e.memset(scanm[:], 1.0), w=scanm)
    for b in range(NB):
        k.pool(lambda e, b=b: e.memset(scanm[:, b * 128:b * 128 + 1], 0.0), w=scanm)
    ebst = xT[:].rearrange("p c n -> p (c n)")[:, 0:2048]
    k.dma(lambda e, s: e.dma_start(out=ebst, in_=bias_d).then_inc(s, 16), xT, w=xT)
    k.dma(lambda e, s: e.dma_start(out=maskt[:], in_=mask_d).then_inc(s, 16), maskt, w=maskt)
    k.act(lambda e: e.activation(out=EBt[:], in_=ebst, func=AF.Exp), r=xT, w=EBt)
    for kvpc in range(4):
        pc_ = kvpc % 2
        k.dve(lambda e, kvpc=kvpc, pc_=pc_: e.tensor_tensor(out=EBt[:, kvpc * 512:(kvpc + 1) * 512].rearrange("p (h n) -> p h n", h=4),
                                                          in0=EBt[:, kvpc * 512:(kvpc + 1) * 512].rearrange("p (h n) -> p h n", h=4),
                                                          in1=maskt[:, pc_ * 128:(pc_ + 1) * 128].unsqueeze(1).to_broadcast([128, 4, 128]), op=ALU.mult),
              r=[EBt, maskt], w=EBt)
    for l in range(L):
        k.dma(lambda e, s, l=l: e.dma_start(out=par[l][:], in_=par_d[l]).then_inc(s, 16), par[l], w=par[l])
        k.dma(lambda e, s, l=l: e.dma_start(out=wa2f[l][:], in_=wa2_d[l]).then_inc(s, 16), wa2f[l], w=wa2f[l])
        k.pool(lambda e, l=l: e.memset(wa2b[l][:], 0.0), w=wa2b[l])
        k.dve(lambda e, l=l: e.tensor_copy(out=wa2b[l][0:16, :], in_=wa2f[l][:]), r=wa2f[l], w=wa2b[l])
        k.dve(lambda e, l=l: e.tensor_scalar(out=der[l][:, 12:14], in0=hm2[:], scalar1=par[l][:, 27:28], scalar2=0.125, op0=ALU.mult, op1=ALU.mult), r=[hm2, par[l]], w=der[l])
        k.dve(lambda e, l=l: e.tensor_scalar(out=der[l][:, 0:1], in0=par[l][:, 24:25], scalar1=-1.0, scalar2=None, op0=ALU.mult), r=par[l], w=der[l])
        k.dve(lambda e, l=l: e.tensor_tensor(out=der[l][:, 1:3], in0=par[l][:, 25:27], in1=par[l][:, 16:18], op=ALU.mult), r=par[l], w=der[l])
        k.dve(lambda e, l=l: e.tensor_scalar(out=der[l][:, 3:4], in0=par[l][:, 27:28], scalar1=0.125, scalar2=None, op0=ALU.mult), r=par[l], w=der[l])
        k.act(lambda e, l=l: e.activation(out=der[l][:, 4:12], in_=par[l][:, 273:281], func=AF.Exp), r=par[l], w=der[l])

    def P(l, a, b=None):
        return par[l][:, a:(a + 1 if b is None else b)]

    hT = k.sb("hT", [128, 8, T], BF16, nsub=8)
    BIG = k.sb("BIG", [128, NJ, T], BF16, nsub=NJ)
    NWK = 8
    wk = [k.sb("wk%d" % i, [128, T], F32) for i in range(NWK)]
    nbk = 6
    bk = [k.sb("bk%d" % i, [128, T], BF16) for i in range(nbk)]
    E1 = k.sb("E1", [128, T], F32)
    E2 = k.sb("E2", [128, T], F32)
    qdT = k.sb("qdT", [128, T], BF16)
    kiT4 = [k.sb("kiT4_%d" % i, [128, T], BF16) for i in range(4)]
    ktT = k.sb("ktT", [128, T], BF16)
    zsb = k.sb("zsb", [128, T], BF16)
    srb = [k.sb("sr%d" % i, [128, T], BF16) for i in range(2)]
    sgc = [k.sb("sgc%d" % i, [128, T], F32) for i in range(2)]
    qk = [k.sb("qk%d" % i, [128, NB, 4, 128], BF16) for i in range(2)]
    attnT = [k.sb("attnT%d" % i, [128, 4, 128], BF16) for i in range(2)]
    PT = [k.sb("PT%d" % i, [128, 4, 128], BF16) for i in range(8)]
    otok = [k.sb("otok%d" % i, [128, 8, 64], BF16) for i in range(2)]
    kttok = [k.sb("kttok%d" % i, [128, 128], BF16) for i in range(2)]
    Vg = [k.sb("Vg%d" % i, [128, 256], BF16) for i in range(NB)]
    den = [k.sb("den%d" % i, [128, 8], F32) for i in range(2)]
    rec = [k.sb("rec%d" % i, [128, 8], F32) for i in range(2)]
    dg = k.sb("dg", [128, 31, 128], BF16)
    fa = [k.sb("fa%d" % i, [128, T], F32) for i in range(4)]
    cor = k.sb("cor", [128, 2 * NJ, 2], F32)
    cort = k.sb("cort", [128, 2 * NJ], F32)
    psb = [k.sb("psb%d" % i, [128, T + 2], BF16) for i in range(4)]
    Sst = [k.sb("S%d" % l, [128, 64], F32) for l in range(L)]
    Sbf = [k.sb("Sbf%d" % l, [128, 4, 64], BF16) for l in range(L)]
    kn = [[k.sb("kn%d_%d" % (l, i), [128, 128 + T], BF16) for i in range(2)] for l in range(L)]
    Vs = [[k.sb("Vs%d_%d" % (l, i), [128, 2, 65], BF16) for i in range(5)] for l in range(L)]
    uT = [[k.sb("uT%d_%d" % (l, i), [128, 30 + T], BF16) for i in range(2)] for l in range(L)]
    pcar = [k.sb("pcar%d" % l, [128, 2, 2 * NJ, 2], F32, nsub=2) for l in range(L)]
    NWR = 4
    wr = [k.sb("wr%d" % i, [128, 8, 512], BF16) for i in range(NWR)]
    NDR = 3
    wdr = [k.sb("wdr%d" % i, [128, NJ, 128], BF16) for i in range(NDR)]
    banks = [k.ps("bank%d" % i, [128, 512], F32) for i in range(8)]
    bank_i = [0]
    rot_banks = [None]

    rot_banks[0] = banks[2:8]

    def nb():
        rot = rot_banks[0]
        b = rot[bank_i[0] % len(rot)]
        bank_i[0] += 1
        return b

    wk_i = [0]

    def nwk():
        t = wk[wk_i[0] % NWK]
        wk_i[0] += 1
        return t

    bk_i = [0]

    def nbk_():
        t = bk[bk_i[0] % nbk]
        bk_i[0] += 1
        return t

    k.pool(lambda e: e.memset(zsb[:], 0.0), w=zsb)
    for l in range(L):
        k.pool(lambda e, l=l: e.memset(Sst[l][:], 0.0), w=Sst[l])
        k.pool(lambda e, l=l: e.memset(Sbf[l][:], 0.0), w=Sbf[l])
        k.pool(lambda e, l=l: e.memset(pcar[l][:], 0.0), w=pcar[l])
        for i in range(2):
            k.pool(lambda e, l=l, i=i: e.memset(uT[l][i][:, 0:30], 0.0), w=uT[l][i])
            k.pool(lambda e, l=l, i=i: e.memset(kn[l][i][:], 0.0), w=kn[l][i])
        for i in range(5):
            k.pool(lambda e, l=l, i=i: e.memset(Vs[l][i][:], 1.0), w=Vs[l][i])


    stg_i = [0]
    pend_st = []

    def flush_stores():
        while pend_st:
            dst, reg, bfb, bv = pend_st.pop(0)
            k.dma(lambda e, s_, dst=dst, bv=bv: e.dma_start(out=dst, in_=bv).then_inc(s_, 16), bfb, r=bfb, w=reg)

    def cast_piece(loads, W, stores, wd=False):
        use_big = (stg_i[0] % 2 == 1)
        st = BIG if use_big else xT
        if use_big:
            flat = BIG[:].rearrange("p a b -> p (a b)").bitcast(F32)[:, 0:8 * T]
        else:
            flat = xT[:].rearrange("p c n -> p (c n)")
        if not wd:
            bfb = wr[stg_i[0] % NWR]
            stg_i[0] += 1
            sv = flat.rearrange("p (c n) -> p c n", c=8)[:, :, 0:W]
            bv = bfb[:, :, 0:W]
            n0, n1 = 4, 7
            tot = 8
        else:
            bfb = wdr[stg_i[0] % NDR]
            stg_i[0] += 1
            sv = flat[:, 0:NJ * 128].rearrange("p (j n) -> p j n", j=NJ)
            bv = bfb[:, :, :]
            n0, n1 = 11, 19
            tot = NJ

        def fl(e, s_):
            for c0, src in loads:
                e.dma_start(out=sv[:, :, c0:c0 + src.shape[2]], in_=src).then_inc(s_, 16)
        k.dma(fl, st, w=st, n=len(loads))
        k.dve(lambda e: e.tensor_copy(out=bv[:, 0:n0, :], in_=sv[:, 0:n0, :]), r=st, w=bfb)
        k.act(lambda e: e.activation(out=bv[:, n0:n1, :], in_=sv[:, n0:n1, :], func=AF.Copy), r=st, w=bfb)
        k.pool(lambda e: e.tensor_copy(out=bv[:, n1:tot, :], in_=sv[:, n1:tot, :]), r=st, w=bfb)
        flush_stores()
        for dst, reg in stores:
            pend_st.append((dst, reg, bfb, bv))

    def v3(ap):
        return ap.rearrange("(c p) n -> p c n", p=128)

    for l in range(L):
        segs = [(0, 256, [0]), (528, 784, [256]), (784, 1296, [512]), (1296, 1360, [1024, 1088]), (1360, 1424, [1152, 1216]),
                (1552, 2064, [1280]), (512, 528, [1792]), (256, 512, [1920]), (1424, 1552, [2176])]
        ri = 0
        for a_, b_, outs in segs:
            W = b_ - a_
            sts = []
            for o in outs:
                sts.append((v3(wins_d[l][:, o:o + W]), Dwin[l].s(ri)))
                ri += 1
            cast_piece([(0, v3(win_d[l][:, a_:b_]))], W, sts)
        for hf in range(2):
            cast_piece([(0, v3(wout_d[l][:, hf * 512:(hf + 1) * 512]))], 512, [(v3(wouts_d[l][:, hf * 512:(hf + 1) * 512]), Dwout[l].s(hf))])
        for G in range(11):
            cast_piece([(0, v3(wup_d[l][:, G * 256:G * 256 + 256])), (256, v3(wup_d[l][:, DFF + G * 256:DFF + G * 256 + 256]))], 512,
                       [(wups_d[l, G], Dwup[l].s(G))])
        for m in range(8):
            cast_piece([(0, wdn_d[l][:, m * 128:(m + 1) * 128].rearrange("(j p) n -> p j n", p=128))], 128,
                       [(wdns_d[l, m], Dwdn[l].s(m))], wd=True)
    flush_stores()
    sched = []
    for t in range(NT):
        for l in range(L):
            sched += [("win", l, 3), ("win", l, 0), ("win", l, 2), ("win", l, 1), ("wv", l, 0), ("wout", l, 0), ("wout", l, 1)]
            sched += [("wup", l, G) for G in range(11)]
    wstate = {"next": 0, "cons": 0}

    def issue_w(i):
        kind, l, idx = sched[i]
        buf = wr[i % NWR]
        if kind == "win":
            if idx < 3:
                src = wins_d[l][:, idx * 512:(idx + 1) * 512].rearrange("(c p) n -> p c n", p=128)
                dst = buf[:, :, :]
            else:
                src = wins_d[l][:, 1536:1808].rearrange("(c p) n -> p c n", p=128)
                dst = buf[:, :, 0:272]
            dep = Dwin[l]
        elif kind == "wv":
            src = wins_d[l][:, 1920:2304].rearrange("(c p) n -> p c n", p=128)
            dst = buf[:, :, 0:384]
            dep = Dwin[l]
        elif kind == "wout":
            src = wouts_d[l][:, idx * 512:(idx + 1) * 512].rearrange("(c p) n -> p c n", p=128)
            dst = buf[:, :, :]
            dep = Dwout[l]
        else:
            src = wups_d[l, idx]
            dst = buf[:, :, :]
            dep = Dwup[l]
        k.dma(lambda e, s: e.dma_start(out=dst, in_=src).then_inc(s, 16), buf, r=dep, w=buf)

    def get_w(kind, l, idx):
        i = wstate["cons"]
        assert sched[i] == (kind, l, idx), (sched[i], kind, l, idx)
        while wstate["next"] < min(len(sched), i + 1):
            issue_w(wstate["next"])
            wstate["next"] += 1
        return wr[i % NWR]

    def done_w():
        wstate["cons"] += 1
        j = wstate["cons"] + NWR - 1
        while wstate["next"] <= j and wstate["next"] < len(sched):
            issue_w(wstate["next"])
            wstate["next"] += 1

    dsched = [(l, m) for t in range(NT) for l in range(L) for m in range(8)]
    dstate = {"next": 0, "cons": 0}

    def issue_d(i):
        l, m = dsched[i]
        buf = wdr[i % NDR]
        k.dma(lambda e, s: e.dma_start(out=buf[:], in_=wdns_d[l, m]).then_inc(s, 16), buf, r=Dwdn[l], w=buf)

    def get_d():
        i = dstate["cons"]
        while dstate["next"] <= i:
            issue_d(dstate["next"])
            dstate["next"] += 1
        return wdr[i % NDR]

    def done_d():
        dstate["cons"] += 1
        j = dstate["cons"] + NDR - 1
        while dstate["next"] <= j and dstate["next"] < len(dsched):
            issue_d(dstate["next"])
            dstate["next"] += 1

    for i in range(min(NWR, len(sched))):
        issue_w(i)
        wstate["next"] += 1

    lnwarm = k.sb("lnwarm", [128, 1], F32)

    def rms_norm(gcol0, l, split_sq=False):
        sq = BIG
        k.act(lambda e: e.activation(out=lnwarm[:], in_=eps_t[:], func=AF.Ln), r=eps_t, w=lnwarm)
        for c in range(8):
            if split_sq and c >= 4:
                k.dve(lambda e, c=c: e.tensor_tensor(out=sq[:, 8 + c, :], in0=xT[:, c, :], in1=xT[:, c, :], op=ALU.mult), r=xT.s(c), w=sq.s(8 + c))
            else:
                k.act(lambda e, c=c: e.activation(out=sq[:, 8 + c, :], in_=xT[:, c, :], func=AF.Square), r=xT.s(c), w=sq.s(8 + c))
        ps = nb()
        for c in range(8):
            k.pe(lambda e, c=c: e.matmul(ps[:], lhsT=ones_b[:], rhs=sq[:, 8 + c, :], start=(c == 0), stop=(c == 7)),
                 r=[ones_b, sq.s(8 + c)], w=ps)
        lnv = nwk()
        rs = nwk()
        k.act(lambda e: e.activation(out=lnv[:], in_=ps[:], func=AF.Ln, bias=eps_t[:, 0:1], scale=1.0 / D), r=[ps, eps_t], w=lnv)
        k.act(lambda e: e.activation(out=rs[:], in_=lnv[:], func=AF.Exp, scale=-0.5), r=lnv, w=rs)
        for c in range(8):
            k.dve(lambda e, c=c: e.scalar_tensor_tensor(out=hT[:, c, :], in0=xT[:, c, :], scalar=P(l, gcol0 + c), in1=rs[:],
                                                        op0=ALU.mult, op1=ALU.mult), r=[xT.s(c), par[l], rs], w=hT.s(c))

    def proj_chunk(wbuf, col0, ncol):
        ps = nb()
        for c in range(8):
            k.pe(lambda e, c=c: e.matmul(ps[0:ncol, :], lhsT=wbuf[:, c, col0:col0 + ncol], rhs=hT[:, c, :], start=(c == 0), stop=(c == 7)),
                 r=[wbuf, hT.s(c)], w=ps)
        return ps

    def head_rsqrt(src_ap_fn, src_deps, n_inv):
        sq = nbk_()
        k.act(lambda e: e.activation(out=sq[:], in_=src_ap_fn(), func=AF.Square), r=src_deps, w=sq)
        ps = nb()
        k.pe(lambda e: e.matmul(ps[:], lhsT=blk64[:], rhs=sq[:], start=True, stop=True), r=[blk64, sq], w=ps)
        lnv = nwk()
        rs = nwk()
        k.act(lambda e: e.activation(out=lnv[:], in_=ps[:], func=AF.Ln, bias=eps_t[:, 0:1], scale=n_inv), r=[ps, eps_t], w=lnv)
        k.act(lambda e: e.activation(out=rs[:], in_=lnv[:], func=AF.Exp, scale=-0.5), r=lnv, w=rs)
        return rs

    class _Stop(Exception):
        pass

    def ckpt(n):
        if _DBG_STAGE == n:
            raise _Stop()

    try:
        for t in range(NT):
            qkf = [qk[i][:].rearrange("p b h n -> p (b h n)").bitcast(F32) for i in range(2)]
            stg = [(E1, E1[:]), (E2, E2[:]), (sgc[0], sgc[0][:]), (sgc[1], sgc[1][:]),
                   (qk[0], qkf[0][:, 0:512]), (qk[0], qkf[0][:, 512:1024]), (qk[1], qkf[1][:, 0:512]), (qk[1], qkf[1][:, 512:1024])]

            def load_x(tt):
                for c in range(8):
                    xt_, xap = stg[c]
                    k.dma(lambda e, s_, xap=xap, c=c: e.dma_start(out=xap, in_=x_d[:, c, tt * T:(tt + 1) * T]).then_inc(s_, 16), xst[c], w=xt_)

            if t == 0:
                load_x(0)
            for c in range(8):
                xt_, xap = stg[c]
                if c % 2 == 0:
                    k.act(lambda e, xap=xap, c=c: e.activation(out=xT[:, c, :], in_=xap, func=AF.Copy), r=xt_, w=xT.s(c))
                else:
                    k.dve(lambda e, xap=xap, c=c: e.tensor_copy(out=xT[:, c, :], in_=xap), r=xt_, w=xT.s(c))

            for l in range(L if _DBG_STAGE != 0 else 0):
                mixT = BIG
                gb0 = t * NB
                rot_banks[0] = banks[2:8]
                rms_norm(0, l, split_sq=(l == 0))
                ckpt(1)
                tasks = []

                def post_z(ps):
                    k.act(lambda e: e.activation(out=zsb[0:16, :], in_=ps[0:16, :], func=AF.Copy), r=ps, w=zsb)
                    lg_ps = nb()
                    k.pe(lambda e: e.matmul(lg_ps[:], lhsT=wa2b[l][:], rhs=zsb[:], start=True, stop=True), r=[wa2b[l], zsb], w=lg_ps)
                    e_t = nwk()
                    k.act(lambda e: e.activation(out=e_t[:], in_=lg_ps[:], func=AF.Exp, bias=der[l][:, 0:1], scale=-1.0), r=[lg_ps, der[l]], w=e_t)
                    lsp = nwk()
                    k.act(lambda e: e.activation(out=lsp[:], in_=e_t[:], func=AF.Ln, bias=one_t[:, 0:1], scale=1.0), r=[e_t, one_t], w=lsp)
                    cum = nwk()
                    k.dve(lambda e: e.tensor_tensor_scan(out=cum[:], data0=scanm[:], data1=lsp[:], initial=0.0, op0=ALU.mult, op1=ALU.add),
                          r=[scanm, lsp], w=cum)
                    k.act(lambda e: e.activation(out=E1[:], in_=cum[:], func=AF.Exp, scale=-1.0 / 16), r=cum, w=E1)
                    k.act(lambda e: e.activation(out=E2[:], in_=cum[:], func=AF.Exp, scale=1.0 / 16), r=cum, w=E2)

                def post_gate(c):
                    def f(ps):
                        k.act(lambda e: e.activation(out=sgc[c][:], in_=ps[:], func=AF.Sigmoid), r=ps, w=sgc[c])
                    return f

                def post_qa(ps):
                    k.dve(lambda e: e.scalar_tensor_tensor(out=qdT[:], in0=ps[:], scalar=32.0 ** -0.5, in1=E1[:], op0=ALU.mult, op1=ALU.mult),
                          r=[ps, E1], w=qdT)

                def post_ka(ps):
                    for h in range(4):
                        k.dve(lambda e, h=h: e.scalar_tensor_tensor(out=kiT4[h][:], in0=ps[:], scalar=hm4[:, h:h + 1], in1=E2[:], op0=ALU.mult, op1=ALU.mult),
                              r=[ps, E2, hm4], w=kiT4[h])
                    for b in range(NB):
                        k.dve(lambda e, b=b: e.scalar_tensor_tensor(out=ktT[:, b * 128:(b + 1) * 128], in0=ps[:, b * 128:(b + 1) * 128],
                                                                    scalar=E1[:, b * 128 + 127:b * 128 + 128], in1=E2[:, b * 128:(b + 1) * 128],
                                                                    op0=ALU.mult, op1=ALU.mult),
                              r=[ps, E1, E2], w=ktT)

                def post_r(c):
                    def f(ps):
                        k.act(lambda e: e.activation(out=srb[c][:], in_=ps[:], func=AF.Silu), r=ps, w=srb[c])
                    return f

                def post_kb(i):
                    def f(ps):
                        knt = kn[l][i]
                        if t > 0:
                            k.pool(lambda e: e.tensor_copy(out=knt[:, 0:128], in_=knt[:, T:T + 128]), r=knt, w=knt)
                        rs = head_rsqrt(lambda: ps[:], [ps], 1.0 / 64)
                        k.dve(lambda e: e.scalar_tensor_tensor(out=knt[:, 128:128 + T], in0=ps[:], scalar=P(l, 28), in1=rs[:],
                                                               op0=ALU.mult, op1=ALU.mult), r=[ps, rs, par[l]], w=knt)
                    return f

                def post_ca(c):
                    def f(ps):
                        ut = uT[l][c]
                        if t > 0:
                            k.pool(lambda e: e.tensor_copy(out=ut[:, 0:30], in_=ut[:, T:T + 30]), r=ut, w=ut)
                        k.dve(lambda e: e.tensor_tensor(out=ut[:, 30:30 + T], in0=ps[:], in1=sgc[c][:], op=ALU.mult), r=[ps, sgc[c]], w=ut)
                    return f

                def post_qb(c):
                    def f(ps):
                        rs = head_rsqrt(lambda: ps[:], [ps], 1.0 / 64)
                        for hh in range(2):
                            kv_, h4_ = c // 2, (c % 2) * 2 + hh
                            k.dve(lambda e, hh=hh, kv_=kv_, h4_=h4_: e.scalar_tensor_tensor(out=qk[kv_][:, :, h4_, :], in0=ps[:].rearrange("p (b n) -> p b n", b=NB),
                                                                                           scalar=der[l][:, 12 + hh:13 + hh], in1=rs[:].rearrange("p (b n) -> p b n", b=NB),
                                                                                           op0=ALU.mult, op1=ALU.mult), r=[ps, rs, der[l]], w=qk[kv_])
                    return f

                tasks += [(3, 256, 16, post_z, False), (3, 0, 128, post_gate(0), False), (3, 128, 128, post_gate(1), True)]
                tasks += [(0, 0, 128, post_qa, False), (0, 128, 128, post_ka, False), (0, 256, 128, post_r(0), False), (0, 384, 128, post_r(1), True)]
                nB1 = len(tasks)
                tasks += [(2, 0, 128, post_kb(0), False), (2, 128, 128, post_kb(1), False), (2, 256, 128, post_ca(0), False), (2, 384, 128, post_ca(1), True)]
                tasks += [(1, c * 128, 128, post_qb(c), c == 3) for c in range(4)]
                cur_piece = [None, None]

                def do_proj(i):
                    pk, c0, nc_, _, last = tasks[i]
                    if cur_piece[0] != pk:
                        cur_piece[0] = pk
                        cur_piece[1] = get_w("win", l, pk)
                    ps = proj_chunk(cur_piece[1], c0, nc_)
                    if last:
                        done_w()
                    return ps

                w3_ = get_w("win", l, 3)
                cur_piece[0], cur_piece[1] = 3, w3_
                first = [nb(), nb(), nb()]
                for c in range(8):
                    for ti in range(3):
                        _, c0_, nc_, _, _ = tasks[ti]
                        k.pe(lambda e, c=c, ti=ti, c0_=c0_, nc_=nc_: e.matmul(first[ti][0:nc_, :], lhsT=w3_[:, c, c0_:c0_ + nc_], rhs=hT[:, c, :],
                                                                             start=(c == 0), stop=(c == 7)), r=[w3_, hT.s(c)], w=first[ti])
                done_w()
                pend = do_proj(3)
                for ti in range(3):
                    tasks[ti][3](first[ti])
                for i in range(3, len(tasks)):
                    nxt = do_proj(i + 1) if i + 1 < len(tasks) else None
                    tasks[i][3](pend)
                    pend = nxt
                wv = get_w("wv", l, 0)
                for b in range(NB):
                    ps = nb()
                    for c in range(8):
                        k.pe(lambda e, c=c, b=b, ps=ps: e.matmul(ps[:, 0:384], lhsT=hT[:, c, b * 128:(b + 1) * 128], rhs=wv[:, c, 0:384],
                                                                 start=(c == 0), stop=(c == 7)), r=[wv, hT.s(c)], w=ps)
                    vs = Vs[l][(gb0 + b) % 5]
                    k.act(lambda e, ps=ps, b=b: e.activation(out=Vg[b][:], in_=ps[:, 0:256], func=AF.Copy), r=ps, w=Vg[b])
                    k.act(lambda e, ps=ps, vs=vs: e.activation(out=vs[:, :, 0:64], in_=ps[:, 256:384].rearrange("p (a d) -> p a d", a=2), func=AF.Copy), r=ps, w=vs)
                done_w()
                ckpt(2)

                oT_ps = [banks[0], banks[1]]

                def conv_diag(c):
                    k.dve(lambda e: e.tensor_tensor(out=dg[:], in0=ident_b[:].unsqueeze(1).to_broadcast([128, 31, 128]),
                                                    in1=par[l][:, 35 + c * 31:35 + (c + 1) * 31].unsqueeze(2).to_broadcast([128, 31, 128]), op=ALU.mult),
                          r=[ident_b, par[l]], w=dg)

                conv_y = {}

                def conv_mm(c):
                    yps = nb()
                    ut = uT[l][c]
                    for kk in range(31):
                        k.pe(lambda e, kk=kk: e.matmul(yps[:], lhsT=dg[:, kk, :], rhs=ut[:, kk:kk + T], start=(kk == 0), stop=(kk == 30)),
                             r=[dg, ut], w=yps)
                    y1 = sgc[c]
                    y2 = psb[c]
                    y3 = psb[2 + c]
                    k.act(lambda e: e.activation(out=y1[:], in_=yps[:], func=AF.Identity, bias=P(l, 29 + c), scale=1.0), r=[yps, par[l]], w=y1)
                    k.act(lambda e: e.activation(out=y3[:, 0:T], in_=yps[:], func=AF.Square, bias=P(l, 29 + c), scale=1.0), r=[yps, par[l]], w=y3)
                    k.act(lambda e: e.activation(out=y2[:, 0:T], in_=y1[:], func=AF.Copy), r=y1, w=y2)
                    conv_y[c] = (y1, y2, y3)

                st = {}

                def swa_scores(b):
                    gb = gb0 + b
                    pcs = [1] if gb == 0 else [0, 1]
                    for kv in range(2):
                        for pc in pcs:
                            sps = nb()
                            kc0 = b * 128 + pc * 128
                            k.pe(lambda e, kv=kv, kc0=kc0, sps=sps: e.matmul(sps[:], lhsT=kn[l][kv][:, kc0:kc0 + 128],
                                                                             rhs=qk[kv][:, b, :, :].rearrange("p h n -> p (h n)"), start=True, stop=True),
                                 r=[kn[l][kv], qk[kv]], w=sps)
                            ex = nbk_()
                            pt = PT[(b % 2) * 4 + kv * 2 + pc]
                            k.act(lambda e, sps=sps, ex=ex: e.activation(out=ex[:], in_=sps[:], func=AF.Exp), r=sps, w=ex)
                            k.dve(lambda e, ex=ex, pt=pt, kv=kv, pc=pc: e.tensor_tensor(out=pt[:].rearrange("p a n -> p (a n)"), in0=ex[:],
                                                                                        in1=EBt[:, (kv * 2 + pc) * 512:(kv * 2 + pc + 1) * 512], op=ALU.mult),
                                  r=[ex, EBt], w=pt)

                def swa_pv(b):
                    gb = gb0 + b
                    pcs = [1] if gb == 0 else [0, 1]
                    pv = [nb(), nb()]
                    for h in range(8):
                        q, hh, kv = h // 2, h % 2, h // 4
                        o_ap = (h // 4, slice((h % 4) * 65, (h % 4) * 65 + 65))
                        first = True
                        for pc in pcs:
                            vsrc = Vs[l][(gb - 1 + pc) % 5]
                            k.pe(lambda e, o_ap=o_ap, h=h, pc=pc, kv=kv, vsrc=vsrc, first=first, last=(pc == 1):
                                 e.matmul(pv[o_ap[0]][:, o_ap[1]], lhsT=PT[(b % 2) * 4 + kv * 2 + pc][:, h % 4, :], rhs=vsrc[:, kv, :], start=first, stop=last),
                                 r=[PT[(b % 2) * 4 + kv * 2 + pc], vsrc], w=pv[o_ap[0]])
                            first = False
                    dn = den[b % 2]
                    rc = rec[b % 2]
                    ot = otok[b % 2]
                    for i in range(2):
                        k.dve(lambda e, i=i: e.tensor_tensor(out=dn[:, 4 * i:4 * i + 4],
                                                             in0=pv[i][:, 0:260].rearrange("p (a d) -> p a d", a=4)[:, :, 64],
                                                             in1=der[l][:, 4 + 4 * i:8 + 4 * i], op=ALU.add), r=[pv[i], der[l]], w=dn)
                    k.dve(lambda e: e.reciprocal(out=rc[:], in_=dn[:]), r=dn, w=rc)
                    for i in range(2):
                        k.dve(lambda e, i=i: e.tensor_tensor(out=ot[:, 4 * i:4 * i + 4, :],
                                                             in0=pv[i][:, 0:260].rearrange("p (a d) -> p a d", a=4)[:, :, 0:64],
                                                             in1=rc[:, 4 * i:4 * i + 4].unsqueeze(2).to_broadcast([128, 4, 64]), op=ALU.mult),
                              r=[pv[i], rc], w=ot)
                    st["ot", b] = ot

                def swa_tr(b):
                    blk = slice(b * 128, (b + 1) * 128)
                    ot = st["ot", b]
                    trp = nb()
                    trb = trp[:].bitcast(BF16)
                    for q in range(4):
                        k.pe(lambda e, q=q: e.transpose(out=trb[:, q * 128:(q + 1) * 128],
                                                        in_=ot[:, 2 * q:2 * q + 2, :].rearrange("p a d -> p (a d)"), identity=ident_b[:]),
                             r=[ot, ident_b], w=trp)
                    k.dve(lambda e: e.tensor_tensor(out=mixT[:, 2:6, blk], in0=trb[:, 0:512].rearrange("p (q n) -> p q n", q=4),
                                                    in1=P(l, 18, 22).unsqueeze(2).to_broadcast([128, 4, 128]), op=ALU.mult),
                          r=[trp, par[l]], w=mixT.s(2, 4))

                def gla_attn(b):
                    blk = slice(b * 128, (b + 1) * 128)
                    ktp = nb()
                    ktb = ktp[:].bitcast(BF16)
                    k.pe(lambda e: e.transpose(out=ktb[:, 0:128], in_=ktT[:, blk], identity=ident_b[:]), r=[ktT, ident_b], w=ktp)
                    ktk = kttok[b % 2]
                    k.act(lambda e: e.activation(out=ktk[:], in_=ktb[:, 0:128], func=AF.Copy), r=ktp, w=ktk)
                    aps = nb()
                    for h in range(4):
                        k.pe(lambda e, h=h: e.matmul(aps[:, h * 128:(h + 1) * 128], lhsT=kiT4[h][:, blk], rhs=qdT[:, blk], start=True, stop=True),
                             r=[kiT4[h], qdT], w=aps)
                    at = attnT[b % 2]
                    k.dve(lambda e: e.tensor_tensor(out=at[:], in0=aps[:].rearrange("p (h n) -> p h n", h=4),
                                                    in1=causalT[:].unsqueeze(1).to_broadcast([128, 4, 128]), op=ALU.mult),
                          r=[aps, causalT], w=at)
                    st["at", b] = (at, ktk)

                def gla_out(b):
                    blk = slice(b * 128, (b + 1) * 128)
                    at, ktk = st["at", b]
                    for h in range(4):
                        c, hh = h // 2, h % 2
                        k.pe(lambda e, h=h, c=c, hh=hh: e.matmul(oT_ps[c][hh * 64:(hh + 1) * 64, blk], lhsT=Vg[b][:, h * 64:(h + 1) * 64],
                                                                 rhs=at[:, h, :], start=True, stop=False, tile_position=(0, hh * 64)),
                             r=[Vg[b], at], w=oT_ps[c])
                        k.pe(lambda e, h=h, c=c, hh=hh: e.matmul(oT_ps[c][hh * 64:(hh + 1) * 64, blk], lhsT=Sbf[l][:, h, :],
                                                                 rhs=qdT[:, blk], start=False, stop=True, tile_position=(0, hh * 64)),
                             r=[Sbf[l], qdT], w=oT_ps[c])
                    dsp = nb()
                    k.pe(lambda e: e.matmul(dsp[:, 0:256], lhsT=ktk[:], rhs=Vg[b][:], start=True, stop=True), r=[ktk, Vg[b]], w=dsp)
                    tmp = nwk()
                    dsd = nwk()
                    k.dve(lambda e: e.tensor_tensor(out=tmp[:, 0:256].rearrange("p (h v) -> p h v", h=4), in0=dsp[:, 0:256].rearrange("p (h v) -> p h v", h=4),
                                                    in1=hm4[:].unsqueeze(2).to_broadcast([128, 4, 64]), op=ALU.mult), r=[dsp, hm4], w=tmp)
                    k.dve(lambda e: e.tensor_reduce(out=dsd[:, 0:64], in_=tmp[:, 0:256].rearrange("p (h v) -> p v h", h=4), axis=mybir.AxisListType.X, op=ALU.add),
                          r=tmp, w=dsd)
                    k.dve(lambda e: e.scalar_tensor_tensor(out=Sst[l][:], in0=Sst[l][:], scalar=E1[:, b * 128 + 127:b * 128 + 128], in1=dsd[:, 0:64],
                                                           op0=ALU.mult, op1=ALU.add), r=[Sst[l], E1, dsd], w=Sst[l])
                    k.dve(lambda e: e.tensor_tensor(out=Sbf[l][:], in0=Sst[l][:].unsqueeze(1).to_broadcast([128, 4, 64]),
                                                    in1=hm4[:].unsqueeze(2).to_broadcast([128, 4, 64]), op=ALU.mult), r=[Sst[l], hm4], w=Sbf[l])

                conv_diag(0)
                swa_scores(0)
                gla_attn(0)
                swa_scores(1)
                for b in range(NB):
                    if b == 1:
                        conv_mm(0)
                        conv_diag(1)
                    if b == 3:
                        conv_mm(1)
                    swa_pv(b)
                    gla_out(b)
                    if b + 2 < NB:
                        swa_scores(b + 2)
                    if b + 1 < NB:
                        gla_attn(b + 1)
                    swa_tr(b)
                ckpt(3)
                for c in range(2):
                    rs = head_rsqrt(lambda c=c: oT_ps[c][:], [oT_ps[c]], 1.0 / 64)
                    on = nwk()
                    k.dve(lambda e, c=c, rs=rs, on=on: e.tensor_tensor(out=on[:], in0=oT_ps[c][:], in1=rs[:], op=ALU.mult), r=[oT_ps[c], rs], w=on)
                    k.dve(lambda e, c=c, on=on: e.scalar_tensor_tensor(out=mixT[:, c, :], in0=on[:], scalar=der[l][:, 1 + c:2 + c], in1=srb[c][:],
                                                                       op0=ALU.mult, op1=ALU.mult), r=[on, der[l], srb[c]], w=mixT.s(c))
                ckpt(4)
                yb = [conv_y[c][0] for c in range(2)]
                ybb = [conv_y[c][1] for c in range(2)]
                sqb = [conv_y[c][2] for c in range(2)]
                s1 = nb()
                s2 = nb()
                for c in range(2):
                    k.pe(lambda e, c=c: e.matmul(s1[:], lhsT=ones_b[:], rhs=ybb[c][:, 0:T], start=(c == 0), stop=(c == 1)), r=[ones_b, ybb[c]], w=s1)
                for c in range(2):
                    k.pe(lambda e, c=c: e.matmul(s2[:], lhsT=ones_b[:], rhs=sqb[c][:, 0:T], start=(c == 0), stop=(c == 1)), r=[ones_b, sqb[c]], w=s2)
                mean = nwk()
                msq = nwk()
                var = nwk()
                k.act(lambda e: e.activation(out=mean[:], in_=s1[:], func=AF.Copy, scale=1.0 / 256), r=s1, w=mean)
                k.dve(lambda e: e.tensor_tensor(out=msq[:], in0=mean[:], in1=mean[:], op=ALU.mult), r=mean, w=msq)
                k.dve(lambda e: e.scalar_tensor_tensor(out=var[:], in0=s2[:], scalar=1.0 / 256, in1=msq[:], op0=ALU.mult, op1=ALU.subtract), r=[s2, msq], w=var)
                lnv = nwk()
                rs = nwk()
                k.act(lambda e: e.activation(out=lnv[:], in_=var[:], func=AF.Ln, bias=eps_t[:, 0:1], scale=1.0), r=[var, eps_t], w=lnv)
                k.act(lambda e: e.activation(out=rs[:], in_=lnv[:], func=AF.Exp, scale=-0.5), r=lnv, w=rs)
                for c in range(2):
                    d1 = yb[c]
                    k.dve(lambda e, d1=d1: e.tensor_tensor(out=d1[:], in0=d1[:], in1=mean[:], op=ALU.subtract), r=[d1, mean], w=d1)
                    k.dve(lambda e, d1=d1: e.tensor_tensor(out=d1[:], in0=d1[:], in1=rs[:], op=ALU.mult), r=[d1, rs], w=d1)
                    k.act(lambda e, d1=d1, c=c: e.activation(out=d1[:], in_=d1[:], func=AF.Silu, bias=P(l, 33 + c), scale=P(l, 31 + c)), r=[d1, par[l]], w=d1)
                    k.act(lambda e, d1=d1, c=c: e.activation(out=mixT[:, 6 + c, :], in_=d1[:], func=AF.Copy, scale=P(l, 22 + c)),
                          r=[d1, par[l]], w=mixT.s(6 + c))

                ckpt(5)
                for half in range(2):
                    wo = get_w("wout", l, half)
                    corder = [2, 3, 4, 5, 0, 1, 6, 7]
                    if half == 0:
                        pss = [nb() for _ in range(4)]
                        for ci, c in enumerate(corder):
                            for mm in range(4):
                                k.pe(lambda e, c=c, ci=ci, mm=mm: e.matmul(pss[mm][:], lhsT=wo[:, c, mm * 128:(mm + 1) * 128], rhs=mixT[:, c, :],
                                                                          start=(ci == 0), stop=(ci == 7)), r=[wo, mixT.s(c)], w=pss[mm])
                        for mm in range(4):
                            k.dve(lambda e, mm=mm: e.tensor_tensor(out=xT[:, mm, :], in0=xT[:, mm, :], in1=pss[mm][:], op=ALU.add), r=[xT.s(mm), pss[mm]], w=xT.s(mm))
                    else:
                        for mm in range(4):
                            m = half * 4 + mm
                            ps = nb()
                            for ci, c in enumerate(corder):
                                k.pe(lambda e, c=c, ci=ci, mm=mm, ps=ps: e.matmul(ps[:], lhsT=wo[:, c, mm * 128:(mm + 1) * 128], rhs=mixT[:, c, :],
                                                                                 start=(ci == 0), stop=(ci == 7)), r=[wo, mixT.s(c)], w=ps)
                            k.dve(lambda e, m=m, ps=ps: e.tensor_tensor(out=xT[:, m, :], in0=xT[:, m, :], in1=ps[:], op=ALU.add), r=[xT.s(m), ps], w=xT.s(m))
                    done_w()

                ckpt(6)
                rot_banks[0] = banks[0:8]
                rms_norm(8, l)
                ckpt(7)
                g = BIG
                chunks = [(G, jj, part) for G in range(11) for jj in range(2) for part in range(2)]
                hstate = {"wu": None}
                pc = pcar[l]
                rd, wrp = (t + 1) % 2, t % 2
                abuf = {}
                wt = par[l][:, 141:273].rearrange("p (i k) -> p i k", k=3)
                pc0 = pc[:, rd, :, 0]
                pc1 = pc[:, rd, :, 1]
                k.pool(lambda e: e.tensor_tensor(out=cor[:, :, 1], in0=wt[:, :, 0], in1=pc1, op=ALU.mult), r=[par[l], pc.s(rd)], w=cor)
                k.pool(lambda e: e.tensor_tensor(out=cort[:], in0=wt[:, :, 1], in1=pc1, op=ALU.mult), r=[par[l], pc.s(rd)], w=cort)
                k.pool(lambda e: e.tensor_tensor(out=cor[:, :, 0], in0=wt[:, :, 0], in1=pc0, op=ALU.mult), r=[par[l], pc.s(rd)], w=cor)
                k.pool(lambda e: e.tensor_tensor(out=cor[:, :, 0], in0=cor[:, :, 0], in1=cort[:], op=ALU.add), r=[cor, cort], w=cor)

                def h_finish(i):
                    G, jj, part = chunks[i]
                    j = 2 * G + jj
                    a = abuf[i]
                    if part == 0:
                        k.act(lambda e: e.activation(out=a[:], in_=a[:], func=AF.Silu), r=a, w=a)
                    else:
                        sg = abuf[i - 1]
                        k.pool(lambda e: e.tensor_tensor(out=g[:, j, :], in0=a[:], in1=sg[:], op=ALU.mult), r=[a, sg], w=g.s(j))

                wu0 = get_w("wup", l, 0)
                hstate["wu"] = wu0
                pre = [nb(), nb()]
                for c in range(8):
                    for part in range(2):
                        k.pe(lambda e, c=c, part=part: e.matmul(pre[part][:], lhsT=wu0[:, c, part * 256:part * 256 + 128], rhs=hT[:, c, :],
                                                               start=(c == 0), stop=(c == 7)), r=[wu0, hT.s(c)], w=pre[part])
                for i, (G, jj, part) in enumerate(chunks):
                    j = 2 * G + jj
                    idx = part * NJ + j
                    if jj == 0 and part == 0 and G > 0:
                        hstate["wu"] = get_w("wup", l, G)
                    if i < 2:
                        pp = pre[i]
                    else:
                        pp = proj_chunk(hstate["wu"], part * 256 + jj * 128, 128)
                    if jj == 1 and part == 1:
                        done_w()
                    a = fa[i % 4]
                    abuf[i] = a
                    w0 = P(l, 141 + idx * 3 + 0)
                    w1 = P(l, 141 + idx * 3 + 1)
                    w2 = P(l, 141 + idx * 3 + 2)
                    k.act(lambda e: e.activation(out=pc[:, wrp, idx, :], in_=pp[:, T - 2:T], func=AF.Copy), r=pp, w=pc.s(wrp))
                    k.act(lambda e: e.activation(out=a[:], in_=pp[:], func=AF.Identity, bias=P(l, 97 + idx), scale=w2), r=[pp, par[l]], w=a)
                    k.dve(lambda e: e.scalar_tensor_tensor(out=a[:, 1:T], in0=pp[:, 0:T - 1], scalar=w1, in1=a[:, 1:T], op0=ALU.mult, op1=ALU.add),
                          r=[pp, a, par[l]], w=a)
                    k.dve(lambda e: e.scalar_tensor_tensor(out=a[:, 2:T], in0=pp[:, 0:T - 2], scalar=w0, in1=a[:, 2:T], op0=ALU.mult, op1=ALU.add),
                          r=[pp, a, par[l]], w=a)
                    k.dve(lambda e: e.tensor_tensor(out=a[:, 0:2], in0=a[:, 0:2], in1=cor[:, idx, :], op=ALU.add), r=[a, cor], w=a)
                    if i > 0:
                        h_finish(i - 1)
                h_finish(len(chunks) - 1)
                if l == L - 1 and t + 1 < NT:
                    load_x(t + 1)
                ckpt(8)
                NI = 3
                i0 = dstate["cons"]
                while dstate["next"] <= i0 + NI - 1:
                    issue_d(dstate["next"])
                    dstate["next"] += 1
                wd2 = [wdr[(i0 + q_) % NDR] for q_ in range(NI)]
                ps2 = [nb() for _ in range(NI)]
                for j in range(NJ):
                    for mm in range(NI):
                        k.pe(lambda e, j=j, mm=mm: e.matmul(ps2[mm][:], lhsT=wd2[mm][:, j, :], rhs=g[:, j, :], start=(j == 0), stop=(j == NJ - 1)),
                             r=[wd2[mm], g.s(j)], w=ps2[mm])
                for mm in range(NI):
                    k.dve(lambda e, mm=mm: e.tensor_tensor(out=xT[:, mm, :], in0=xT[:, mm, :], in1=ps2[mm][:], op=ALU.add), r=[xT.s(mm), ps2[mm]], w=xT.s(mm))
                for _ in range(NI):
                    done_d()
                for m in range(NI, 8):
                    wd = get_d()
                    ps = nb()
                    for j in range(NJ):
                        k.pe(lambda e, j=j, ps=ps, wd=wd: e.matmul(ps[:], lhsT=wd[:, j, :], rhs=g[:, j, :], start=(j == 0), stop=(j == NJ - 1)),
                             r=[wd, g.s(j)], w=ps)
                    k.dve(lambda e, m=m, ps=ps: e.tensor_tensor(out=xT[:, m, :], in0=xT[:, m, :], in1=ps[:], op=ALU.add), r=[xT.s(m), ps], w=xT.s(m))
                    done_d()

            for c in range(8):
                xo = wk[c]
                if c % 2 == 0:
                    k.dve(lambda e, c=c, xo=xo: e.tensor_copy(out=xo[:], in_=xT[:, c, :]), r=xT.s(c), w=xo)
                else:
                    k.act(lambda e, c=c, xo=xo: e.activation(out=xo[:], in_=xT[:, c, :], func=AF.Copy), r=xT.s(c), w=xo)
                k.dma(lambda e, s_, c=c, xo=xo: e.dma_start(out=out_d[:, c, t * T:(t + 1) * T], in_=xo[:]).then_inc(s_, 16),
                      xo, r=xo, w=Dout.s(t * 8 + c))
    except _Stop:
        pass
    k.op("sp", lambda e: e.nop(), r=Dout)
    k.finalize()
    return nc, k


def _t5_bucket_np(dist):
    max_exact = 16
    d = np.maximum(dist, 0)
    d_f = np.maximum(d, 1).astype(np.float32)
    large = max_exact + (np.log(d_f / max_exact) / np.float32(np.log(128 / max_exact)) * (32 - max_exact)).astype(np.int32)
    large = np.minimum(large, 31)
    return np.where(d < max_exact, d, large)


def _pack_params(inp, L):
    f = np.float32
    P = np.zeros((L, 128, NPAR), f)
    for l in range(L):
        P[l, :, 0:8] = inp["attn_norm_g"][l].reshape(8, 128).T
        P[l, :, 8:16] = inp["ffn_norm_g"][l].reshape(8, 128).T
        P[l, :, 16:24] = inp["branch_scale"][l].reshape(8, 128).T
        P[l, :, 24] = inp["gla_b_a"][l]
        P[l, :, 25:27] = inp["gla_out_g"][l].reshape(2, 128).T
        P[l, :, 27] = np.tile(inp["swa_q_g"][l], 2)
        P[l, :, 28] = np.tile(inp["swa_k_g"][l], 2)
        P[l, :, 29:31] = inp["conv_dw_b"][l].reshape(2, 128).T
        P[l, :, 31:33] = inp["conv_ln_g"][l].reshape(2, 128).T
        P[l, :, 33:35] = inp["conv_ln_b"][l].reshape(2, 128).T
        P[l, :, 35:97] = inp["conv_dw_w"][l].reshape(31, 2, 128).transpose(2, 1, 0).reshape(128, 62)
        P[l, :, 97:141] = inp["ffn_conv_b"][l].reshape(44, 128).T
        P[l, :, 141:273] = inp["ffn_conv_w"][l].reshape(3, 44, 128).transpose(2, 1, 0).reshape(128, 132)
        P[l, :, 273:281] = np.tile(inp["swa_sinks"][l][None, :], (128, 1))
    return P


def _band_tables(rel_bias):
    j = np.arange(128)[:, None, None]
    pc = np.arange(2)[None, :, None]
    i = np.arange(128)[None, None, :]
    dist = (1 - pc) * 128 + i - j
    band = (dist >= 0) & (dist < 128)
    idx = _t5_bucket_np(dist)
    bt = np.asarray(rel_bias, np.float32)[idx]
    bt = bt.transpose(0, 3, 1, 2).reshape(128, 2, 4, 2, 128)
    bt = np.ascontiguousarray(bt.transpose(0, 1, 3, 2, 4)).reshape(128, 8 * 2 * 128)
    mk = np.ascontiguousarray(np.broadcast_to(band, (128, 2, 128))).astype(np.float32).reshape(128, 256)
    return bt, mk


_CACHE = {}


def _prep(inputs, L):
    f = np.float32
    inp = {k_: np.asarray(v) for k_, v in inputs.items()}
    bt, mk = _band_tables(inp["rel_bias"])
    common = {
        "w_in": np.ascontiguousarray(inp["w_in"][:L], f),
        "w_out": np.ascontiguousarray(inp["w_out"][:L], f),
        "w_up": np.ascontiguousarray(inp["w_up"][:L], f),
        "w_down": np.ascontiguousarray(inp["w_down"][:L], f),
        "wa2": np.ascontiguousarray(inp["gla_w_a2"][:L], f),
        "params": _pack_params(inp, L),
        "biasT": bt,
        "maskT": mk,
    }
    return inp, common


def kernel(**inputs):
    L = 2
    inp, common = _prep(inputs, L)
    x = np.asarray(inp["x"], np.float32)
    B, S, _ = x.shape
    key = (S, L)
    if key not in _CACHE:
        _CACHE[key] = build(S, L)[0]
    nc = _CACHE[key]
    in_maps = []
    for c in range(B):
        m = dict(common)
        m["x"] = np.ascontiguousarray(x[c].reshape(S, 8, 128).transpose(2, 1, 0))
        in_maps.append(m)
    res = run_bass_kernel_spmd(nc, in_maps, core_ids=list(range(B)))
    out = np.stack([np.asarray(r["out"], np.float32).transpose(2, 1, 0).reshape(S, D) for r in res.results], axis=0)
    return out
```

```python
import numpy as np
import concourse.bass as bass
import concourse.mybir as mybir
from concourse.bass_utils import run_bass_kernel_spmd
from contextlib import ExitStack

F32 = mybir.dt.float32
BF16 = mybir.dt.bfloat16
AF = mybir.ActivationFunctionType
ALU = mybir.AluOpType

ENGS = ("pe", "act", "dve", "pool", "sp")
SEM_LIMIT = 6000

D = 1024
T = 512
NB = T // 128
DIN = 2064
DFF = 2816
NJ = DFF // 128
EPS = 1e-6
NPAR = 281
WCOLS = 2304
import os as _os
_DBG_STAGE = int(_os.environ.get("KSTAGE", "99"))


class Tile:
    _n = 0

    def __init__(self, t, nsub=1, name="", excl=False):
        self.t = t
        self.nsub = nsub
        self.name = name
        self.excl = excl
        Tile._n += 1
        self.id = Tile._n
        self.dma_sem = None
        self.dma_cnt = 0

    def __getitem__(self, k):
        return self.t[k]

    def all(self):
        return [(self.id, i) for i in range(self.nsub)]

    def s(self, i, n=1):
        return [(self.id, j) for j in range(i, i + n)]


def regs(x):
    out = []
    if x is None:
        return out
    if isinstance(x, Tile):
        return x.all()
    if isinstance(x, tuple) and len(x) == 2 and isinstance(x[0], int):
        return [x]
    for e in x:
        out.extend(regs(e))
    return out


class _RecIns:
    def __init__(self, c):
        self.c = c

    def then_inc(self, sem, v):
        self.c[3] = (sem, v)
        return self


class _Rec:
    def __init__(self):
        self.calls = []

    def __getattr__(self, name):
        def f(*a, **kw):
            c = [name, a, kw, None]
            self.calls.append(c)
            return _RecIns(c)
        return f


class Op:
    __slots__ = ("eng", "calls", "deps", "has_dep", "tok", "dma_tile", "ndma", "dma_val")

    def __init__(self, eng, fn):
        self.eng = eng
        self.calls = None
        self.deps = []
        self.has_dep = False
        self.tok = None
        self.dma_tile = None
        self.ndma = 0
        self.dma_val = 0


class KB:
    def __init__(self, nc):
        self.nc = nc
        self.es = ExitStack()
        self.ops = {e: [] for e in ENGS}
        self.last_w = {}
        self.readers = {}
        self.nsem = 0
        self.sb_bytes = 0
        self.excl_ids = set()

    def sb(self, name, shape, dtype, nsub=1):
        t = self.es.enter_context(self.nc.sbuf_tensor(name, list(shape), dtype))
        n = 1
        for d in shape[1:]:
            n *= d
        self.sb_bytes += n * (2 if dtype == BF16 else 4)
        return Tile(t, nsub, name)

    def ps(self, name, shape, dtype):
        t = self.es.enter_context(self.nc.psum_tensor(name, list(shape), dtype))
        tl = Tile(t, 1, name, excl=True)
        self.excl_ids.add(tl.id)
        return tl

    def sem(self, name):
        self.nsem += 1
        return self.es.enter_context(self.nc.semaphore(name))

    def op(self, eng, fn, r=None, w=None, dma_tile=None, ndma=0):
        o = Op(eng, fn)
        rr = regs(r)
        ww = regs(w)
        ex = [q for q in rr if q[0] in self.excl_ids and q not in ww]
        if ex:
            ww = ww + ex
        deps = []
        for k in rr:
            lw = self.last_w.get(k)
            if lw is not None:
                deps.append(lw)
        for k in ww:
            lw = self.last_w.get(k)
            if lw is not None:
                deps.append(lw)
            deps.extend(self.readers.get(k, ()))
        seen = set()
        for d in deps:
            if id(d) in seen:
                continue
            seen.add(id(d))
            if d.eng == eng and d.ndma == 0 and eng in ("pe", "sp"):
                continue
            o.deps.append(d)
            d.has_dep = True
        for k in rr:
            self.readers.setdefault(k, []).append(o)
        for k in ww:
            self.last_w[k] = o
            self.readers[k] = []
        if ndma:
            o.ndma = ndma
            o.dma_tile = dma_tile
            if dma_tile.dma_sem is None:
                dma_tile.dma_sem = self.sem("d%d_%s" % (dma_tile.id, dma_tile.name))
            dma_tile.dma_cnt += 16 * ndma
            o.dma_val = dma_tile.dma_cnt
            rec = _Rec()
            fn(rec, dma_tile.dma_sem)
            assert len(rec.calls) == ndma
        else:
            rec = _Rec()
            fn(rec)
            assert len(rec.calls) == 1
        o.calls = rec.calls
        self.ops[eng].append(o)
        return o

    def pe(self, fn, r=None, w=None):
        return self.op("pe", fn, r, w)

    def act(self, fn, r=None, w=None):
        return self.op("act", fn, r, w)

    def dve(self, fn, r=None, w=None):
        return self.op("dve", fn, r, w)

    def pool(self, fn, r=None, w=None):
        return self.op("pool", fn, r, w)

    def dma(self, fn, tile, r=None, w=None, n=1, eng="sp"):
        return self.op(eng, fn, r, w, dma_tile=tile, ndma=n)

    def finalize(self):
        for e in ENGS:
            cnt = 0
            cur = None
            for o in self.ops[e]:
                if o.ndma:
                    o.tok = (o.dma_tile.dma_sem, o.dma_val)
                    continue
                if not o.has_dep:
                    continue
                if cur is None or cnt >= SEM_LIMIT:
                    cur = self.sem("e_%s_%d" % (e, self.nsem))
                    cnt = 0
                cnt += 1
                o.tok = (cur, cnt)
        stats = {e: [0, 0] for e in ENGS}
        ops = self.ops

        def run(e, engine):
            known = {}
            for o in ops[e]:
                need = {}
                for d in o.deps:
                    s, v = d.tok
                    if known.get(id(s), 0) >= v:
                        continue
                    if need.get(id(s), (None, 0))[1] < v:
                        need[id(s)] = (s, v)
                for s, v in need.values():
                    engine.wait_ge(s, v)
                    known[id(s)] = v
                    stats[e][1] += 1
                if o.ndma:
                    for name, a, kw, inc in o.calls:
                        getattr(engine, name)(*a, **kw).then_inc(inc[0], inc[1])
                else:
                    name, a, kw, _ = o.calls[0]
                    ins = getattr(engine, name)(*a, **kw)
                    if o.tok is not None:
                        ins.then_inc(o.tok[0], 1)
                stats[e][0] += 1

        with self.nc.Block() as block:
            @block.tensor
            def _(eng):
                run("pe", eng)

            @block.scalar
            def _(eng):
                run("act", eng)

            @block.vector
            def _(eng):
                run("dve", eng)

            @block.gpsimd
            def _(eng):
                run("pool", eng)

            @block.sync
            def _(eng):
                run("sp", eng)
        self.stats = stats
        self.es.close()


def build(S, L=2, dbg=False):
    NT = S // T
    nc = bass.Bass("TRN2", target_bir_lowering=False)
    x_d = nc.dram_tensor("x", [128, 8, S], F32, kind="ExternalInput").ap()
    win_d = nc.dram_tensor("w_in", [L, D, DIN], F32, kind="ExternalInput").ap()
    wout_d = nc.dram_tensor("w_out", [L, D, D], F32, kind="ExternalInput").ap()
    wup_d = nc.dram_tensor("w_up", [L, D, 2 * DFF], F32, kind="ExternalInput").ap()
    wdn_d = nc.dram_tensor("w_down", [L, DFF, D], F32, kind="ExternalInput").ap()
    wa2_d = nc.dram_tensor("wa2", [L, 16, 128], F32, kind="ExternalInput").ap()
    par_d = nc.dram_tensor("params", [L, 128, NPAR], F32, kind="ExternalInput").ap()
    bias_d = nc.dram_tensor("biasT", [128, 8 * 2 * 128], F32, kind="ExternalInput").ap()
    mask_d = nc.dram_tensor("maskT", [128, 2 * 128], F32, kind="ExternalInput").ap()
    out_d = nc.dram_tensor("out", [128, 8, S], F32, kind="ExternalOutput").ap()
    if dbg:
        dbg_d = nc.dram_tensor("dbg", [128, 8 * T], BF16, kind="ExternalOutput").ap()
        Ddbg = Tile(None, 1, "dbg")
    wins_d = nc.dram_tensor("wins", [L, D, WCOLS], BF16, kind="Internal").ap()
    wouts_d = nc.dram_tensor("wouts", [L, D, D], BF16, kind="Internal").ap()
    wups_d = nc.dram_tensor("wups", [L, 11, 128, 8, 512], BF16, kind="Internal").ap()
    wdns_d = nc.dram_tensor("wdns", [L, 8, 128, NJ, 128], BF16, kind="Internal").ap()

    k = KB(nc)
    Dwin = [Tile(None, 11, "wins%d" % l) for l in range(L)]
    Dwout = [Tile(None, 2, "wouts%d" % l) for l in range(L)]
    Dwup = [Tile(None, 11, "wups%d" % l) for l in range(L)]
    Dwdn = [Tile(None, 8, "wdns%d" % l) for l in range(L)]
    Dout = Tile(None, NT * 8, "out")
    xst = [Tile(None, 1, "xst%d" % c) for c in range(8)]
    xo_t = [Tile(None, 1, "xo%d" % c) for c in range(8)]

    xT = k.sb("xT", [128, 8, T], F32, nsub=8)
    ident_f = k.sb("ident_f", [128, 128], F32)
    ident_b = k.sb("ident_b", [128, 128], BF16)
    ones_f = k.sb("ones_f", [128, 128], F32)
    ones_b = k.sb("ones_b", [128, 128], BF16)
    blk64 = k.sb("blk64", [128, 128], BF16)
    causalT = k.sb("causalT", [128, 128], F32)
    scanm = k.sb("scanm", [128, T], F32)
    EBt = k.sb("EBt", [128, 8 * 2 * 128], BF16)
    maskt = k.sb("maskt", [128, 2 * 128], F32)
    eps_t = k.sb("eps_t", [128, 1], F32)
    one_t = k.sb("one_t", [128, 1], F32)
    par = [k.sb("par%d" % l, [128, NPAR], F32) for l in range(L)]
    der = [k.sb("der%d" % l, [128, 16], F32) for l in range(L)]
    wa2f = [k.sb("wa2f%d" % l, [16, 128], F32) for l in range(L)]
    wa2b = [k.sb("wa2b%d" % l, [128, 128], BF16) for l in range(L)]

    k.pool(lambda e: e.memset(ones_f[:], 1.0), w=ones_f)
    k.pool(lambda e: e.memset(eps_t[:], EPS), w=eps_t)
    k.pool(lambda e: e.memset(one_t[:], 1.0), w=one_t)
    k.pool(lambda e: e.affine_select(out=ident_f[:], in_=ones_f[:], pattern=[[-1, 128]], compare_op=ALU.is_equal,
                                     fill=0.0, base=0, channel_multiplier=1), r=ones_f, w=ident_f)
    k.pool(lambda e: e.affine_select(out=causalT[:], in_=ones_f[:], pattern=[[1, 128]], compare_op=ALU.is_ge,
                                     fill=0.0, base=0, channel_multiplier=-1), r=ones_f, w=causalT)
    k.dve(lambda e: e.tensor_copy(out=ident_b[:], in_=ident_f[:]), r=ident_f, w=ident_b)
    k.dve(lambda e: e.tensor_copy(out=ones_b[:], in_=ones_f[:]), r=ones_f, w=ones_b)
    k.pool(lambda e: e.memset(blk64[:], 0.0), w=blk64)
    k.pool(lambda e: e.memset(blk64[0:64, 0:64], 1.0), w=blk64)
    k.pool(lambda e: e.memset(blk64[64:128, 64:128], 1.0), w=blk64)
    hm4 = k.sb("hm4", [128, 4], F32)
    hm2 = k.sb("hm2", [128, 2], F32)
    for hm_, w_, n_ in ((hm4, 32, 4), (hm2, 64, 2)):
        k.pool(lambda e, hm_=hm_, w_=w_, n_=n_: e.affine_select(out=hm_[:], in_=ones_f[:, 0:n_], pattern=[[-w_, n_]], compare_op=ALU.is_ge,
                                                            fill=0.0, base=0, channel_multiplier=1), r=ones_f, w=hm_)
        k.pool(lambda e, hm_=hm_, w_=w_, n_=n_: e.affine_select(out=hm_[:], in_=hm_[:], pattern=[[w_, n_]], compare_op=ALU.is_ge,
                                                            fill=0.0, base=w_ - 1, channel_multiplier=-1), r=hm_, w=hm_)
    k.pool(lambda e: e.memset(scanm[:], 1.0), w=scanm)
    for b in range(NB):
        k.pool(lambda e, b=b: e.memset(scanm[:, b * 128:b * 128 + 1], 0.0), w=scanm)
    ebst = xT[:].rearrange("p c n -> p (c n)")[:, 0:2048]
    k.dma(lambda e, s: e.dma_start(out=ebst, in_=bias_d).then_inc(s, 16), xT, w=xT)
    k.dma(lambda e, s: e.dma_start(out=maskt[:], in_=mask_d).then_inc(s, 16), maskt, w=maskt)
    k.act(lambda e: e.activation(out=EBt[:], in_=ebst, func=AF.Exp), r=xT, w=EBt)
    for kvpc in range(4):
        pc_ = kvpc % 2
        k.dve(lambda e, kvpc=kvpc, pc_=pc_: e.tensor_tensor(out=EBt[:, kvpc * 512:(kvpc + 1) * 512].rearrange("p (h n) -> p h n", h=4),
                                                          in0=EBt[:, kvpc * 512:(kvpc + 1) * 512].rearrange("p (h n) -> p h n", h=4),
                                                          in1=maskt[:, pc_ * 128:(pc_ + 1) * 128].unsqueeze(1).to_broadcast([128, 4, 128]), op=ALU.mult),
              r=[EBt, maskt], w=EBt)
    for l in range(L):
        k.dma(lambda e, s, l=l: e.dma_start(out=par[l][:], in_=par_d[l]).then_inc(s, 16), par[l], w=par[l])
        k.dma(lambda e, s, l=l: e.dma_start(out=wa2f[l][:], in_=wa2_d[l]).then_inc(s, 16), wa2f[l], w=wa2f[l])
        k.pool(lambda e, l=l: e.memset(wa2b[l][:], 0.0), w=wa2b[l])
        k.dve(lambda e, l=l: e.tensor_copy(out=wa2b[l][0:16, :], in_=wa2f[l][:]), r=wa2f[l], w=wa2b[l])
        k.dve(lambda e, l=l: e.tensor_scalar(out=der[l][:, 12:14], in0=hm2[:], scalar1=par[l][:, 27:28], scalar2=0.125, op0=ALU.mult, op1=ALU.mult), r=[hm2, par[l]], w=der[l])
        k.dve(lambda e, l=l: e.tensor_scalar(out=der[l][:, 0:1], in0=par[l][:, 24:25], scalar1=-1.0, scalar2=None, op0=ALU.mult), r=par[l], w=der[l])
        k.dve(lambda e, l=l: e.tensor_tensor(out=der[l][:, 1:3], in0=par[l][:, 25:27], in1=par[l][:, 16:18], op=ALU.mult), r=par[l], w=der[l])
        k.dve(lambda e, l=l: e.tensor_scalar(out=der[l][:, 3:4], in0=par[l][:, 27:28], scalar1=0.125, scalar2=None, op0=ALU.mult), r=par[l], w=der[l])
        k.act(lambda e, l=l: e.activation(out=der[l][:, 4:12], in_=par[l][:, 273:281], func=AF.Exp), r=par[l], w=der[l])

    def P(l, a, b=None):
        return par[l][:, a:(a + 1 if b is None else b)]

    hT = k.sb("hT", [128, 8, T], BF16, nsub=8)
    BIG = k.sb("BIG", [128, NJ, T], BF16, nsub=NJ)
    NWK = 8
    wk = [k.sb("wk%d" % i, [128, T], F32) for i in range(NWK)]
    nbk = 6
    bk = [k.sb("bk%d" % i, [128, T], BF16) for i in range(nbk)]
    E1 = k.sb("E1", [128, T], F32)
    E2 = k.sb("E2", [128, T], F32)
    qdT = k.sb("qdT", [128, T], BF16)
    kiT4 = [k.sb("kiT4_%d" % i, [128, T], BF16) for i in range(4)]
    ktT = k.sb("ktT", [128, T], BF16)
    zsb = k.sb("zsb", [128, T], BF16)
    srb = [k.sb("sr%d" % i, [128, T], BF16) for i in range(2)]
    sgc = [k.sb("sgc%d" % i, [128, T], F32) for i in range(2)]
    qk = [k.sb("qk%d" % i, [128, NB, 4, 128], BF16) for i in range(2)]
    attnT = [k.sb("attnT%d" % i, [128, 4, 128], BF16) for i in range(2)]
    PT = [k.sb("PT%d" % i, [128, 4, 128], BF16) for i in range(8)]
    otok = [k.sb("otok%d" % i, [128, 8, 64], BF16) for i in range(2)]
    kttok = [k.sb("kttok%d" % i, [128, 128], BF16) for i in range(2)]
    Vg = [k.sb("Vg%d" % i, [128, 256], BF16) for i in range(NB)]
    den = [k.sb("den%d" % i, [128, 8], F32) for i in range(2)]
    rec = [k.sb("rec%d" % i, [128, 8], F32) for i in range(2)]
    dg = k.sb("dg", [128, 31, 128], BF16)
    fa = [k.sb("fa%d" % i, [128, T], F32) for i in range(4)]
    cor = k.sb("cor", [128, 2 * NJ, 2], F32)
    cort = k.sb("cort", [128, 2 * NJ], F32)
    psb = [k.sb("psb%d" % i, [128, T + 2], BF16) for i in range(4)]
    Sst = [k.sb("S%d" % l, [128, 64], F32) for l in range(L)]
    Sbf = [k.sb("Sbf%d" % l, [128, 4, 64], BF16) for l in range(L)]
    kn = [[k.sb("kn%d_%d" % (l, i), [128, 128 + T], BF16) for i in range(2)] for l in range(L)]
    Vs = [[k.sb("Vs%d_%d" % (l, i), [128, 2, 65], BF16) for i in range(5)] for l in range(L)]
    uT = [[k.sb("uT%d_%d" % (l, i), [128, 30 + T], BF16) for i in range(2)] for l in range(L)]
    pcar = [k.sb("pcar%d" % l, [128, 2, 2 * NJ, 2], F32, nsub=2) for l in range(L)]
    NWR = 4
    wr = [k.sb("wr%d" % i, [128, 8, 512], BF16) for i in range(NWR)]
    NDR = 3
    wdr = [k.sb("wdr%d" % i, [128, NJ, 128], BF16) for i in range(NDR)]
    banks = [k.ps("bank%d" % i, [128, 512], F32) for i in range(8)]
    bank_i = [0]
    rot_banks = [None]

    rot_banks[0] = banks[2:8]

    def nb():
        rot = rot_banks[0]
        b = rot[bank_i[0] % len(rot)]
        bank_i[0] += 1
        return b

    wk_i = [0]

    def nwk():
        t = wk[wk_i[0] % NWK]
        wk_i[0] += 1
        return t

    bk_i = [0]

    def nbk_():
        t = bk[bk_i[0] % nbk]
        bk_i[0] += 1
        return t

    k.pool(lambda e: e.memset(zsb[:], 0.0), w=zsb)
    for l in range(L):
        k.pool(lambda e, l=l: e.memset(Sst[l][:], 0.0), w=Sst[l])
        k.pool(lambda e, l=l: e.memset(Sbf[l][:], 0.0), w=Sbf[l])
        k.pool(lambda e, l=l: e.memset(pcar[l][:], 0.0), w=pcar[l])
        for i in range(2):
            k.pool(lambda e, l=l, i=i: e.memset(uT[l][i][:, 0:30], 0.0), w=uT[l][i])
            k.pool(lambda e, l=l, i=i: e.memset(kn[l][i][:], 0.0), w=kn[l][i])
        for i in range(5):
            k.pool(lambda e, l=l, i=i: e.memset(Vs[l][i][:], 1.0), w=Vs[l][i])


    stg_i = [0]
    pend_st = []

    def flush_stores():
        while pend_st:
            dst, reg, bfb, bv = pend_st.pop(0)
            k.dma(lambda e, s_, dst=dst, bv=bv: e.dma_start(out=dst, in_=bv).then_inc(s_, 16), bfb, r=bfb, w=reg)

    def cast_piece(loads, W, stores, wd=False):
        use_big = (stg_i[0] % 2 == 1)
        st = BIG if use_big else xT
        if use_big:
            flat = BIG[:].rearrange("p a b -> p (a b)").bitcast(F32)[:, 0:8 * T]
        else:
            flat = xT[:].rearrange("p c n -> p (c n)")
        if not wd:
            bfb = wr[stg_i[0] % NWR]
            stg_i[0] += 1
            sv = flat.rearrange("p (c n) -> p c n", c=8)[:, :, 0:W]
            bv = bfb[:, :, 0:W]
            n0, n1 = 4, 7
            tot = 8
        else:
            bfb = wdr[stg_i[0] % NDR]
            stg_i[0] += 1
            sv = flat[:, 0:NJ * 128].rearrange("p (j n) -> p j n", j=NJ)
            bv = bfb[:, :, :]
            n0, n1 = 11, 19
            tot = NJ

        def fl(e, s_):
            for c0, src in loads:
                e.dma_start(out=sv[:, :, c0:c0 + src.shape[2]], in_=src).then_inc(s_, 16)
        k.dma(fl, st, w=st, n=len(loads))
        k.dve(lambda e: e.tensor_copy(out=bv[:, 0:n0, :], in_=sv[:, 0:n0, :]), r=st, w=bfb)
        k.act(lambda e: e.activation(out=bv[:, n0:n1, :], in_=sv[:, n0:n1, :], func=AF.Copy), r=st, w=bfb)
        k.pool(lambda e: e.tensor_copy(out=bv[:, n1:tot, :], in_=sv[:, n1:tot, :]), r=st, w=bfb)
        flush_stores()
        for dst, reg in stores:
            pend_st.append((dst, reg, bfb, bv))

    def v3(ap):
        return ap.rearrange("(c p) n -> p c n", p=128)

    for l in range(L):
        segs = [(0, 256, [0]), (528, 784, [256]), (784, 1296, [512]), (1296, 1360, [1024, 1088]), (1360, 1424, [1152, 1216]),
                (1552, 2064, [1280]), (512, 528, [1792]), (256, 512, [1920]), (1424, 1552, [2176])]
        ri = 0
        for a_, b_, outs in segs:
            W = b_ - a_
            sts = []
            for o in outs:
                sts.append((v3(wins_d[l][:, o:o + W]), Dwin[l].s(ri)))
                ri += 1
            cast_piece([(0, v3(win_d[l][:, a_:b_]))], W, sts)
        for hf in range(2):
            cast_piece([(0, v3(wout_d[l][:, hf * 512:(hf + 1) * 512]))], 512, [(v3(wouts_d[l][:, hf * 512:(hf + 1) * 512]), Dwout[l].s(hf))])
        for G in range(11):
            cast_piece([(0, v3(wup_d[l][:, G * 256:G * 256 + 256])), (256, v3(wup_d[l][:, DFF + G * 256:DFF + G * 256 + 256]))], 512,
                       [(wups_d[l, G], Dwup[l].s(G))])
        for m in range(8):
            cast_piece([(0, wdn_d[l][:, m * 128:(m + 1) * 128].rearrange("(j p) n -> p j n", p=128))], 128,
                       [(wdns_d[l, m], Dwdn[l].s(m))], wd=True)
    flush_stores()
    sched = []
    for t in range(NT):
        for l in range(L):
            sched += [("win", l, 3), ("win", l, 0), ("win", l, 2), ("win", l, 1), ("wv", l, 0), ("wout", l, 0), ("wout", l, 1)]
            sched += [("wup", l, G) for G in range(11)]
    wstate = {"next": 0, "cons": 0}

    def issue_w(i):
        kind, l, idx = sched[i]
        buf = wr[i % NWR]
        if kind == "win":
            if idx < 3:
                src = wins_d[l][:, idx * 512:(idx + 1) * 512].rearrange("(c p) n -> p c n", p=128)
                dst = buf[:, :, :]
            else:
                src = wins_d[l][:, 1536:1808].rearrange("(c p) n -> p c n", p=128)
                dst = buf[:, :, 0:272]
            dep = Dwin[l]
        elif kind == "wv":
            src = wins_d[l][:, 1920:2304].rearrange("(c p) n -> p c n", p=128)
            dst = buf[:, :, 0:384]
            dep = Dwin[l]
        elif kind == "wout":
            src = wouts_d[l][:, idx * 512:(idx + 1) * 512].rearrange("(c p) n -> p c n", p=128)
            dst = buf[:, :, :]
            dep = Dwout[l]
        else:
            src = wups_d[l, idx]
            dst = buf[:, :, :]
            dep = Dwup[l]
        k.dma(lambda e, s: e.dma_start(out=dst, in_=src).then_inc(s, 16), buf, r=dep, w=buf)

    def get_w(kind, l, idx):
        i = wstate["cons"]
        assert sched[i] == (kind, l, idx), (sched[i], kind, l, idx)
        while wstate["next"] < min(len(sched), i + 1):
            issue_w(wstate["next"])
            wstate["next"] += 1
        return wr[i % NWR]

    def done_w():
        wstate["cons"] += 1
        j = wstate["cons"] + NWR - 1
        while wstate["next"] <= j and wstate["next"] < len(sched):
            issue_w(wstate["next"])
            wstate["next"] += 1

    dsched = [(l, m) for t in range(NT) for l in range(L) for m in range(8)]
    dstate = {"next": 0, "cons": 0}

    def issue_d(i):
        l, m = dsched[i]
        buf = wdr[i % NDR]
        k.dma(lambda e, s: e.dma_start(out=buf[:], in_=wdns_d[l, m]).then_inc(s, 16), buf, r=Dwdn[l], w=buf)

    def get_d():
        i = dstate["cons"]
        while dstate["next"] <= i:
            issue_d(dstate["next"])
            dstate["next"] += 1
        return wdr[i % NDR]

    def done_d():
        dstate["cons"] += 1
        j = dstate["cons"] + NDR - 1
        while dstate["next"] <= j and dstate["next"] < len(dsched):
            issue_d(dstate["next"])
            dstate["next"] += 1

    for i in range(min(NWR, len(sched))):
        issue_w(i)
        wstate["next"] += 1

    lnwarm = k.sb("lnwarm", [128, 1], F32)

    def rms_norm(gcol0, l, split_sq=False):
        sq = BIG
        k.act(lambda e: e.activation(out=lnwarm[:], in_=eps_t[:], func=AF.Ln), r=eps_t, w=lnwarm)
        for c in range(8):
            if split_sq and c >= 4:
                k.dve(lambda e, c=c: e.tensor_tensor(out=sq[:, 8 + c, :], in0=xT[:, c, :], in1=xT[:, c, :], op=ALU.mult), r=xT.s(c), w=sq.s(8 + c))
            else:
                k.act(lambda e, c=c: e.activation(out=sq[:, 8 + c, :], in_=xT[:, c, :], func=AF.Square), r=xT.s(c), w=sq.s(8 + c))
        ps = nb()
        for c in range(8):
            k.pe(lambda e, c=c: e.matmul(ps[:], lhsT=ones_b[:], rhs=sq[:, 8 + c, :], start=(c == 0), stop=(c == 7)),
                 r=[ones_b, sq.s(8 + c)], w=ps)
        lnv = nwk()
        rs = nwk()
        k.act(lambda e: e.activation(out=lnv[:], in_=ps[:], func=AF.Ln, bias=eps_t[:, 0:1], scale=1.0 / D), r=[ps, eps_t], w=lnv)
        k.act(lambda e: e.activation(out=rs[:], in_=lnv[:], func=AF.Exp, scale=-0.5), r=lnv, w=rs)
        for c in range(8):
            k.dve(lambda e, c=c: e.scalar_tensor_tensor(out=hT[:, c, :], in0=xT[:, c, :], scalar=P(l, gcol0 + c), in1=rs[:],
                                                        op0=ALU.mult, op1=ALU.mult), r=[xT.s(c), par[l], rs], w=hT.s(c))

    def proj_chunk(wbuf, col0, ncol):
        ps = nb()
        for c in range(8):
            k.pe(lambda e, c=c: e.matmul(ps[0:ncol, :], lhsT=wbuf[:, c, col0:col0 + ncol], rhs=hT[:, c, :], start=(c == 0), stop=(c == 7)),
                 r=[wbuf, hT.s(c)], w=ps)
        return ps

    def head_rsqrt(src_ap_fn, src_deps, n_inv):
        sq = nbk_()
        k.act(lambda e: e.activation(out=sq[:], in_=src_ap_fn(), func=AF.Square), r=src_deps, w=sq)
        ps = nb()
        k.pe(lambda e: e.matmul(ps[:], lhsT=blk64[:], rhs=sq[:], start=True, stop=True), r=[blk64, sq], w=ps)
        lnv = nwk()
        rs = nwk()
        k.act(lambda e: e.activation(out=lnv[:], in_=ps[:], func=AF.Ln, bias=eps_t[:, 0:1], scale=n_inv), r=[ps, eps_t], w=lnv)
        k.act(lambda e: e.activation(out=rs[:], in_=lnv[:], func=AF.Exp, scale=-0.5), r=lnv, w=rs)
        return rs

    class _Stop(Exception):
        pass

    def ckpt(n):
        if _DBG_STAGE == n:
            raise _Stop()

    try:
        for t in range(NT):
            qkf = [qk[i][:].rearrange("p b h n -> p (b h n)").bitcast(F32) for i in range(2)]
            stg = [(E1, E1[:]), (E2, E2[:]), (sgc[0], sgc[0][:]), (sgc[1], sgc[1][:]),
                   (qk[0], qkf[0][:, 0:512]), (qk[0], qkf[0][:, 512:1024]), (qk[1], qkf[1][:, 0:512]), (qk[1], qkf[1][:, 512:1024])]

            def load_x(tt):
                for c in range(8):
                    xt_, xap = stg[c]
                    k.dma(lambda e, s_, xap=xap, c=c: e.dma_start(out=xap, in_=x_d[:, c, tt * T:(tt + 1) * T]).then_inc(s_, 16), xst[c], w=xt_)

            if t == 0:
                load_x(0)
            for c in range(8):
                xt_, xap = stg[c]
                if c % 2 == 0:
                    k.act(lambda e, xap=xap, c=c: e.activation(out=xT[:, c, :], in_=xap, func=AF.Copy), r=xt_, w=xT.s(c))
                else:
                    k.dve(lambda e, xap=xap, c=c: e.tensor_copy(out=xT[:, c, :], in_=xap), r=xt_, w=xT.s(c))

            for l in range(L if _DBG_STAGE != 0 else 0):
                mixT = BIG
                gb0 = t * NB
                rot_banks[0] = banks[2:8]
                rms_norm(0, l, split_sq=(l == 0))
                ckpt(1)
                tasks = []

                def post_z(ps):
                    k.act(lambda e: e.activation(out=zsb[0:16, :], in_=ps[0:16, :], func=AF.Copy), r=ps, w=zsb)
                    lg_ps = nb()
                    k.pe(lambda e: e.matmul(lg_ps[:], lhsT=wa2b[l][:], rhs=zsb[:], start=True, stop=True), r=[wa2b[l], zsb], w=lg_ps)
                    e_t = nwk()
                    k.act(lambda e: e.activation(out=e_t[:], in_=lg_ps[:], func=AF.Exp, bias=der[l][:, 0:1], scale=-1.0), r=[lg_ps, der[l]], w=e_t)
                    lsp = nwk()
                    k.act(lambda e: e.activation(out=lsp[:], in_=e_t[:], func=AF.Ln, bias=one_t[:, 0:1], scale=1.0), r=[e_t, one_t], w=lsp)
                    cum = nwk()
                    k.dve(lambda e: e.tensor_tensor_scan(out=cum[:], data0=scanm[:], data1=lsp[:], initial=0.0, op0=ALU.mult, op1=ALU.add),
                          r=[scanm, lsp], w=cum)
                    k.act(lambda e: e.activation(out=E1[:], in_=cum[:], func=AF.Exp, scale=-1.0 / 16), r=cum, w=E1)
                    k.act(lambda e: e.activation(out=E2[:], in_=cum[:], func=AF.Exp, scale=1.0 / 16), r=cum, w=E2)

                def post_gate(c):
                    def f(ps):
                        k.act(lambda e: e.activation(out=sgc[c][:], in_=ps[:], func=AF.Sigmoid), r=ps, w=sgc[c])
                    return f

                def post_qa(ps):
                    k.dve(lambda e: e.scalar_tensor_tensor(out=qdT[:], in0=ps[:], scalar=32.0 ** -0.5, in1=E1[:], op0=ALU.mult, op1=ALU.mult),
                          r=[ps, E1], w=qdT)

                def post_ka(ps):
                    for h in range(4):
                        k.dve(lambda e, h=h: e.scalar_tensor_tensor(out=kiT4[h][:], in0=ps[:], scalar=hm4[:, h:h + 1], in1=E2[:], op0=ALU.mult, op1=ALU.mult),
                              r=[ps, E2, hm4], w=kiT4[h])
                    for b in range(NB):
                        k.dve(lambda e, b=b: e.scalar_tensor_tensor(out=ktT[:, b * 128:(b + 1) * 128], in0=ps[:, b * 128:(b + 1) * 128],
                                                                    scalar=E1[:, b * 128 + 127:b * 128 + 128], in1=E2[:, b * 128:(b + 1) * 128],
                                                                    op0=ALU.mult, op1=ALU.mult),
                              r=[ps, E1, E2], w=ktT)

                def post_r(c):
                    def f(ps):
                        k.act(lambda e: e.activation(out=srb[c][:], in_=ps[:], func=AF.Silu), r=ps, w=srb[c])
                    return f

                def post_kb(i):
                    def f(ps):
                        knt = kn[l][i]
                        if t > 0:
                            k.pool(lambda e: e.tensor_copy(out=knt[:, 0:128], in_=knt[:, T:T + 128]), r=knt, w=knt)
                        rs = head_rsqrt(lambda: ps[:], [ps], 1.0 / 64)
                        k.dve(lambda e: e.scalar_tensor_tensor(out=knt[:, 128:128 + T], in0=ps[:], scalar=P(l, 28), in1=rs[:],
                                                               op0=ALU.mult, op1=ALU.mult), r=[ps, rs, par[l]], w=knt)
                    return f

                def post_ca(c):
                    def f(ps):
                        ut = uT[l][c]
                        if t > 0:
                            k.pool(lambda e: e.tensor_copy(out=ut[:, 0:30], in_=ut[:, T:T + 30]), r=ut, w=ut)
                        k.dve(lambda e: e.tensor_tensor(out=ut[:, 30:30 + T], in0=ps[:], in1=sgc[c][:], op=ALU.mult), r=[ps, sgc[c]], w=ut)
                    return f

                def post_qb(c):
                    def f(ps):
                        rs = head_rsqrt(lambda: ps[:], [ps], 1.0 / 64)
                        for hh in range(2):
                            kv_, h4_ = c // 2, (c % 2) * 2 + hh
                            k.dve(lambda e, hh=hh, kv_=kv_, h4_=h4_: e.scalar_tensor_tensor(out=qk[kv_][:, :, h4_, :], in0=ps[:].rearrange("p (b n) -> p b n", b=NB),
                                                                                           scalar=der[l][:, 12 + hh:13 + hh], in1=rs[:].rearrange("p (b n) -> p b n", b=NB),
                                                                                           op0=ALU.mult, op1=ALU.mult), r=[ps, rs, der[l]], w=qk[kv_])
                    return f

                tasks += [(3, 256, 16, post_z, False), (3, 0, 128, post_gate(0), False), (3, 128, 128, post_gate(1), True)]
                tasks += [(0, 0, 128, post_qa, False), (0, 128, 128, post_ka, False), (0, 256, 128, post_r(0), False), (0, 384, 128, post_r(1), True)]
                nB1 = len(tasks)
                tasks += [(2, 0, 128, post_kb(0), False), (2, 128, 128, post_kb(1), False), (2, 256, 128, post_ca(0), False), (2, 384, 128, post_ca(1), True)]
                tasks += [(1, c * 128, 128, post_qb(c), c == 3) for c in range(4)]
                cur_piece = [None, None]

                def do_proj(i):
                    pk, c0, nc_, _, last = tasks[i]
                    if cur_piece[0] != pk:
                        cur_piece[0] = pk
                        cur_piece[1] = get_w("win", l, pk)
                    ps = proj_chunk(cur_piece[1], c0, nc_)
                    if last:
                        done_w()
                    return ps

                w3_ = get_w("win", l, 3)
                cur_piece[0], cur_piece[1] = 3, w3_
                first = [nb(), nb(), nb()]
                for c in range(8):
                    for ti in range(3):
                        _, c0_, nc_, _, _ = tasks[ti]
                        k.pe(lambda e, c=c, ti=ti, c0_=c0_, nc_=nc_: e.matmul(first[ti][0:nc_, :], lhsT=w3_[:, c, c0_:c0_ + nc_], rhs=hT[:, c, :],
                                                                             start=(c == 0), stop=(c == 7)), r=[w3_, hT.s(c)], w=first[ti])
                done_w()
                pend = do_proj(3)
                for ti in range(3):
                    tasks[ti][3](first[ti])
                for i in range(3, len(tasks)):
                    nxt = do_proj(i + 1) if i + 1 < len(tasks) else None
                    tasks[i][3](pend)
                    pend = nxt
                wv = get_w("wv", l, 0)
                for b in range(NB):
                    ps = nb()
                    for c in range(8):
                        k.pe(lambda e, c=c, b=b, ps=ps: e.matmul(ps[:, 0:384], lhsT=hT[:, c, b * 128:(b + 1) * 128], rhs=wv[:, c, 0:384],
                                                                 start=(c == 0), stop=(c == 7)), r=[wv, hT.s(c)], w=ps)
                    vs = Vs[l][(gb0 + b) % 5]
                    k.act(lambda e, ps=ps, b=b: e.activation(out=Vg[b][:], in_=ps[:, 0:256], func=AF.Copy), r=ps, w=Vg[b])
                    k.act(lambda e, ps=ps, vs=vs: e.activation(out=vs[:, :, 0:64], in_=ps[:, 256:384].rearrange("p (a d) -> p a d", a=2), func=AF.Copy), r=ps, w=vs)
                done_w()
                ckpt(2)

                oT_ps = [banks[0], banks[1]]

                def conv_diag(c):
                    k.dve(lambda e: e.tensor_tensor(out=dg[:], in0=ident_b[:].unsqueeze(1).to_broadcast([128, 31, 128]),
                                                    in1=par[l][:, 35 + c * 31:35 + (c + 1) * 31].unsqueeze(2).to_broadcast([128, 31, 128]), op=ALU.mult),
                          r=[ident_b, par[l]], w=dg)

                conv_y = {}

                def conv_mm(c):
                    yps = nb()
                    ut = uT[l][c]
                    for kk in range(31):
                        k.pe(lambda e, kk=kk: e.matmul(yps[:], lhsT=dg[:, kk, :], rhs=ut[:, kk:kk + T], start=(kk == 0), stop=(kk == 30)),
                             r=[dg, ut], w=yps)
                    y1 = sgc[c]
                    y2 = psb[c]
                    y3 = psb[2 + c]
                    k.act(lambda e: e.activation(out=y1[:], in_=yps[:], func=AF.Identity, bias=P(l, 29 + c), scale=1.0), r=[yps, par[l]], w=y1)
                    k.act(lambda e: e.activation(out=y3[:, 0:T], in_=yps[:], func=AF.Square, bias=P(l, 29 + c), scale=1.0), r=[yps, par[l]], w=y3)
                    k.act(lambda e: e.activation(out=y2[:, 0:T], in_=y1[:], func=AF.Copy), r=y1, w=y2)
                    conv_y[c] = (y1, y2, y3)

                st = {}

                def swa_scores(b):
                    gb = gb0 + b
                    pcs = [1] if gb == 0 else [0, 1]
                    for kv in range(2):
                        for pc in pcs:
                            sps = nb()
                            kc0 = b * 128 + pc * 128
                            k.pe(lambda e, kv=kv, kc0=kc0, sps=sps: e.matmul(sps[:], lhsT=kn[l][kv][:, kc0:kc0 + 128],
                                                                             rhs=qk[kv][:, b, :, :].rearrange("p h n -> p (h n)"), start=True, stop=True),
                                 r=[kn[l][kv], qk[kv]], w=sps)
                            ex = nbk_()
                            pt = PT[(b % 2) * 4 + kv * 2 + pc]
                            k.act(lambda e, sps=sps, ex=ex: e.activation(out=ex[:], in_=sps[:], func=AF.Exp), r=sps, w=ex)
                            k.pool(lambda e, ex=ex, pt=pt, kv=kv, pc=pc: e.tensor_tensor(out=pt[:].rearrange("p a n -> p (a n)"), in0=ex[:],
                                                                                         in1=EBt[:, (kv * 2 + pc) * 512:(kv * 2 + pc + 1) * 512], op=ALU.mult),
                                   r=[ex, EBt], w=pt)

                def swa_pv(b):
                    gb = gb0 + b
                    pcs = [1] if gb == 0 else [0, 1]
                    pv = [nb(), nb()]
                    for h in range(8):
                        q, hh, kv = h // 2, h % 2, h // 4
                        o_ap = (h // 4, slice((h % 4) * 65, (h % 4) * 65 + 65))
                        first = True
                        for pc in pcs:
                            vsrc = Vs[l][(gb - 1 + pc) % 5]
                            k.pe(lambda e, o_ap=o_ap, h=h, pc=pc, kv=kv, vsrc=vsrc, first=first, last=(pc == 1):
                                 e.matmul(pv[o_ap[0]][:, o_ap[1]], lhsT=PT[(b % 2) * 4 + kv * 2 + pc][:, h % 4, :], rhs=vsrc[:, kv, :], start=first, stop=last),
                                 r=[PT[(b % 2) * 4 + kv * 2 + pc], vsrc], w=pv[o_ap[0]])
                            first = False
                    dn = den[b % 2]
                    rc = rec[b % 2]
                    ot = otok[b % 2]
                    for i in range(2):
                        k.dve(lambda e, i=i: e.tensor_tensor(out=dn[:, 4 * i:4 * i + 4],
                                                             in0=pv[i][:, 0:260].rearrange("p (a d) -> p a d", a=4)[:, :, 64],
                                                             in1=der[l][:, 4 + 4 * i:8 + 4 * i], op=ALU.add), r=[pv[i], der[l]], w=dn)
                    k.dve(lambda e: e.reciprocal(out=rc[:], in_=dn[:]), r=dn, w=rc)
                    for i in range(2):
                        k.dve(lambda e, i=i: e.tensor_tensor(out=ot[:, 4 * i:4 * i + 4, :],
                                                             in0=pv[i][:, 0:260].rearrange("p (a d) -> p a d", a=4)[:, :, 0:64],
                                                             in1=rc[:, 4 * i:4 * i + 4].unsqueeze(2).to_broadcast([128, 4, 64]), op=ALU.mult),
                              r=[pv[i], rc], w=ot)
                    st["ot", b] = ot

                def swa_tr(b):
                    blk = slice(b * 128, (b + 1) * 128)
                    ot = st["ot", b]
                    trp = nb()
                    trb = trp[:].bitcast(BF16)
                    for q in range(4):
                        k.pe(lambda e, q=q: e.transpose(out=trb[:, q * 128:(q + 1) * 128],
                                                        in_=ot[:, 2 * q:2 * q + 2, :].rearrange("p a d -> p (a d)"), identity=ident_b[:]),
                             r=[ot, ident_b], w=trp)
                    k.dve(lambda e: e.tensor_tensor(out=mixT[:, 2:6, blk], in0=trb[:, 0:512].rearrange("p (q n) -> p q n", q=4),
                                                    in1=P(l, 18, 22).unsqueeze(2).to_broadcast([128, 4, 128]), op=ALU.mult),
                          r=[trp, par[l]], w=mixT.s(2, 4))

                def gla_attn(b):
                    blk = slice(b * 128, (b + 1) * 128)
                    ktp = nb()
                    ktb = ktp[:].bitcast(BF16)
                    k.pe(lambda e: e.transpose(out=ktb[:, 0:128], in_=ktT[:, blk], identity=ident_b[:]), r=[ktT, ident_b], w=ktp)
                    ktk = kttok[b % 2]
                    k.act(lambda e: e.activation(out=ktk[:], in_=ktb[:, 0:128], func=AF.Copy), r=ktp, w=ktk)
                    aps = nb()
                    for h in range(4):
                        k.pe(lambda e, h=h: e.matmul(aps[:, h * 128:(h + 1) * 128], lhsT=kiT4[h][:, blk], rhs=qdT[:, blk], start=True, stop=True),
                             r=[kiT4[h], qdT], w=aps)
                    at = attnT[b % 2]
                    k.dve(lambda e: e.tensor_tensor(out=at[:], in0=aps[:].rearrange("p (h n) -> p h n", h=4),
                                                    in1=causalT[:].unsqueeze(1).to_broadcast([128, 4, 128]), op=ALU.mult),
                          r=[aps, causalT], w=at)
                    st["at", b] = (at, ktk)

                def gla_out(b):
                    blk = slice(b * 128, (b + 1) * 128)
                    at, ktk = st["at", b]
                    for h in range(4):
                        c, hh = h // 2, h % 2
                        k.pe(lambda e, h=h, c=c, hh=hh: e.matmul(oT_ps[c][hh * 64:(hh + 1) * 64, blk], lhsT=Vg[b][:, h * 64:(h + 1) * 64],
                                                                 rhs=at[:, h, :], start=True, stop=False, tile_position=(0, hh * 64)),
                             r=[Vg[b], at], w=oT_ps[c])
                        k.pe(lambda e, h=h, c=c, hh=hh: e.matmul(oT_ps[c][hh * 64:(hh + 1) * 64, blk], lhsT=Sbf[l][:, h, :],
                                                                 rhs=qdT[:, blk], start=False, stop=True, tile_position=(0, hh * 64)),
                             r=[Sbf[l], qdT], w=oT_ps[c])
                    dsp = nb()
                    k.pe(lambda e: e.matmul(dsp[:, 0:256], lhsT=ktk[:], rhs=Vg[b][:], start=True, stop=True), r=[ktk, Vg[b]], w=dsp)
                    tmp = nwk()
                    dsd = nwk()
                    k.dve(lambda e: e.tensor_tensor(out=tmp[:, 0:256].rearrange("p (h v) -> p h v", h=4), in0=dsp[:, 0:256].rearrange("p (h v) -> p h v", h=4),
                                                    in1=hm4[:].unsqueeze(2).to_broadcast([128, 4, 64]), op=ALU.mult), r=[dsp, hm4], w=tmp)
                    k.dve(lambda e: e.tensor_reduce(out=dsd[:, 0:64], in_=tmp[:, 0:256].rearrange("p (h v) -> p v h", h=4), axis=mybir.AxisListType.X, op=ALU.add),
                          r=tmp, w=dsd)
                    k.dve(lambda e: e.scalar_tensor_tensor(out=Sst[l][:], in0=Sst[l][:], scalar=E1[:, b * 128 + 127:b * 128 + 128], in1=dsd[:, 0:64],
                                                           op0=ALU.mult, op1=ALU.add), r=[Sst[l], E1, dsd], w=Sst[l])
                    k.dve(lambda e: e.tensor_tensor(out=Sbf[l][:], in0=Sst[l][:].unsqueeze(1).to_broadcast([128, 4, 64]),
                                                    in1=hm4[:].unsqueeze(2).to_broadcast([128, 4, 64]), op=ALU.mult), r=[Sst[l], hm4], w=Sbf[l])

                conv_diag(0)
                swa_scores(0)
                gla_attn(0)
                swa_scores(1)
                for b in range(NB):
                    if b == 1:
                        conv_mm(0)
                        conv_diag(1)
                    if b == 3:
                        conv_mm(1)
                    swa_pv(b)
                    gla_out(b)
                    if b + 2 < NB:
                        swa_scores(b + 2)
                    if b + 1 < NB:
                        gla_attn(b + 1)
                    swa_tr(b)
                ckpt(3)
                for c in range(2):
                    rs = head_rsqrt(lambda c=c: oT_ps[c][:], [oT_ps[c]], 1.0 / 64)
                    on = nwk()
                    k.dve(lambda e, c=c, rs=rs, on=on: e.tensor_tensor(out=on[:], in0=oT_ps[c][:], in1=rs[:], op=ALU.mult), r=[oT_ps[c], rs], w=on)
                    k.dve(lambda e, c=c, on=on: e.scalar_tensor_tensor(out=mixT[:, c, :], in0=on[:], scalar=der[l][:, 1 + c:2 + c], in1=srb[c][:],
                                                                       op0=ALU.mult, op1=ALU.mult), r=[on, der[l], srb[c]], w=mixT.s(c))
                ckpt(4)
                yb = [conv_y[c][0] for c in range(2)]
                ybb = [conv_y[c][1] for c in range(2)]
                sqb = [conv_y[c][2] for c in range(2)]
                s1 = nb()
                s2 = nb()
                for c in range(2):
                    k.pe(lambda e, c=c: e.matmul(s1[:], lhsT=ones_b[:], rhs=ybb[c][:, 0:T], start=(c == 0), stop=(c == 1)), r=[ones_b, ybb[c]], w=s1)
                for c in range(2):
                    k.pe(lambda e, c=c: e.matmul(s2[:], lhsT=ones_b[:], rhs=sqb[c][:, 0:T], start=(c == 0), stop=(c == 1)), r=[ones_b, sqb[c]], w=s2)
                mean = nwk()
                msq = nwk()
                var = nwk()
                k.act(lambda e: e.activation(out=mean[:], in_=s1[:], func=AF.Copy, scale=1.0 / 256), r=s1, w=mean)
                k.dve(lambda e: e.tensor_tensor(out=msq[:], in0=mean[:], in1=mean[:], op=ALU.mult), r=mean, w=msq)
                k.dve(lambda e: e.scalar_tensor_tensor(out=var[:], in0=s2[:], scalar=1.0 / 256, in1=msq[:], op0=ALU.mult, op1=ALU.subtract), r=[s2, msq], w=var)
                lnv = nwk()
                rs = nwk()
                k.act(lambda e: e.activation(out=lnv[:], in_=var[:], func=AF.Ln, bias=eps_t[:, 0:1], scale=1.0), r=[var, eps_t], w=lnv)
                k.act(lambda e: e.activation(out=rs[:], in_=lnv[:], func=AF.Exp, scale=-0.5), r=lnv, w=rs)
                for c in range(2):
                    d1 = yb[c]
                    k.dve(lambda e, d1=d1: e.tensor_tensor(out=d1[:], in0=d1[:], in1=mean[:], op=ALU.subtract), r=[d1, mean], w=d1)
                    k.dve(lambda e, d1=d1: e.tensor_tensor(out=d1[:], in0=d1[:], in1=rs[:], op=ALU.mult), r=[d1, rs], w=d1)
                    k.act(lambda e, d1=d1, c=c: e.activation(out=d1[:], in_=d1[:], func=AF.Silu, bias=P(l, 33 + c), scale=P(l, 31 + c)), r=[d1, par[l]], w=d1)
                    k.act(lambda e, d1=d1, c=c: e.activation(out=mixT[:, 6 + c, :], in_=d1[:], func=AF.Copy, scale=P(l, 22 + c)),
                          r=[d1, par[l]], w=mixT.s(6 + c))

                ckpt(5)
                for half in range(2):
                    wo = get_w("wout", l, half)
                    corder = [2, 3, 4, 5, 0, 1, 6, 7]
                    if half == 0:
                        pss = [nb() for _ in range(4)]
                        for ci, c in enumerate(corder):
                            for mm in range(4):
                                k.pe(lambda e, c=c, ci=ci, mm=mm: e.matmul(pss[mm][:], lhsT=wo[:, c, mm * 128:(mm + 1) * 128], rhs=mixT[:, c, :],
                                                                          start=(ci == 0), stop=(ci == 7)), r=[wo, mixT.s(c)], w=pss[mm])
                        for mm in range(4):
                            k.dve(lambda e, mm=mm: e.tensor_tensor(out=xT[:, mm, :], in0=xT[:, mm, :], in1=pss[mm][:], op=ALU.add), r=[xT.s(mm), pss[mm]], w=xT.s(mm))
                    else:
                        for mm in range(4):
                            m = half * 4 + mm
                            ps = nb()
                            for ci, c in enumerate(corder):
                                k.pe(lambda e, c=c, ci=ci, mm=mm, ps=ps: e.matmul(ps[:], lhsT=wo[:, c, mm * 128:(mm + 1) * 128], rhs=mixT[:, c, :],
                                                                                 start=(ci == 0), stop=(ci == 7)), r=[wo, mixT.s(c)], w=ps)
                            k.dve(lambda e, m=m, ps=ps: e.tensor_tensor(out=xT[:, m, :], in0=xT[:, m, :], in1=ps[:], op=ALU.add), r=[xT.s(m), ps], w=xT.s(m))
                    done_w()

                ckpt(6)
                rot_banks[0] = banks[0:8]
                rms_norm(8, l)
                ckpt(7)
                g = BIG
                chunks = [(G, jj, part) for G in range(11) for jj in range(2) for part in range(2)]
                hstate = {"wu": None}
                pc = pcar[l]
                rd, wrp = (t + 1) % 2, t % 2
                abuf = {}
                wt = par[l][:, 141:273].rearrange("p (i k) -> p i k", k=3)
                pc0 = pc[:, rd, :, 0]
                pc1 = pc[:, rd, :, 1]
                k.pool(lambda e: e.tensor_tensor(out=cor[:, :, 1], in0=wt[:, :, 0], in1=pc1, op=ALU.mult), r=[par[l], pc.s(rd)], w=cor)
                k.pool(lambda e: e.tensor_tensor(out=cort[:], in0=wt[:, :, 1], in1=pc1, op=ALU.mult), r=[par[l], pc.s(rd)], w=cort)
                k.pool(lambda e: e.tensor_tensor(out=cor[:, :, 0], in0=wt[:, :, 0], in1=pc0, op=ALU.mult), r=[par[l], pc.s(rd)], w=cor)
                k.pool(lambda e: e.tensor_tensor(out=cor[:, :, 0], in0=cor[:, :, 0], in1=cort[:], op=ALU.add), r=[cor, cort], w=cor)

                def h_finish(i):
                    G, jj, part = chunks[i]
                    j = 2 * G + jj
                    a = abuf[i]
                    if part == 0:
                        k.act(lambda e: e.activation(out=a[:], in_=a[:], func=AF.Silu), r=a, w=a)
                    else:
                        sg = abuf[i - 1]
                        k.pool(lambda e: e.tensor_tensor(out=g[:, j, :], in0=a[:], in1=sg[:], op=ALU.mult), r=[a, sg], w=g.s(j))

                wu0 = get_w("wup", l, 0)
                hstate["wu"] = wu0
                pre = [nb(), nb()]
                for c in range(8):
                    for part in range(2):
                        k.pe(lambda e, c=c, part=part: e.matmul(pre[part][:], lhsT=wu0[:, c, part * 256:part * 256 + 128], rhs=hT[:, c, :],
                                                               start=(c == 0), stop=(c == 7)), r=[wu0, hT.s(c)], w=pre[part])
                for i, (G, jj, part) in enumerate(chunks):
                    j = 2 * G + jj
                    idx = part * NJ + j
                    if jj == 0 and part == 0 and G > 0:
                        hstate["wu"] = get_w("wup", l, G)
                    if i < 2:
                        pp = pre[i]
                    else:
                        pp = proj_chunk(hstate["wu"], part * 256 + jj * 128, 128)
                    if jj == 1 and part == 1:
                        done_w()
                    a = fa[i % 4]
                    abuf[i] = a
                    w0 = P(l, 141 + idx * 3 + 0)
                    w1 = P(l, 141 + idx * 3 + 1)
                    w2 = P(l, 141 + idx * 3 + 2)
                    k.act(lambda e: e.activation(out=pc[:, wrp, idx, :], in_=pp[:, T - 2:T], func=AF.Copy), r=pp, w=pc.s(wrp))
                    k.act(lambda e: e.activation(out=a[:], in_=pp[:], func=AF.Identity, bias=P(l, 97 + idx), scale=w2), r=[pp, par[l]], w=a)
                    k.dve(lambda e: e.scalar_tensor_tensor(out=a[:, 1:T], in0=pp[:, 0:T - 1], scalar=w1, in1=a[:, 1:T], op0=ALU.mult, op1=ALU.add),
                          r=[pp, a, par[l]], w=a)
                    k.dve(lambda e: e.scalar_tensor_tensor(out=a[:, 2:T], in0=pp[:, 0:T - 2], scalar=w0, in1=a[:, 2:T], op0=ALU.mult, op1=ALU.add),
                          r=[pp, a, par[l]], w=a)
                    k.dve(lambda e: e.tensor_tensor(out=a[:, 0:2], in0=a[:, 0:2], in1=cor[:, idx, :], op=ALU.add), r=[a, cor], w=a)
                    if i > 0:
                        h_finish(i - 1)
                h_finish(len(chunks) - 1)
                if l == L - 1 and t + 1 < NT:
                    load_x(t + 1)
                ckpt(8)
                NI = 3
                i0 = dstate["cons"]
                while dstate["next"] <= i0 + NI - 1:
                    issue_d(dstate["next"])
                    dstate["next"] += 1
                wd2 = [wdr[(i0 + q_) % NDR] for q_ in range(NI)]
                ps2 = [nb() for _ in range(NI)]
                for j in range(NJ):
                    for mm in range(NI):
                        k.pe(lambda e, j=j, mm=mm: e.matmul(ps2[mm][:], lhsT=wd2[mm][:, j, :], rhs=g[:, j, :], start=(j == 0), stop=(j == NJ - 1)),
                             r=[wd2[mm], g.s(j)], w=ps2[mm])
                for mm in range(NI):
                    k.dve(lambda e, mm=mm: e.tensor_tensor(out=xT[:, mm, :], in0=xT[:, mm, :], in1=ps2[mm][:], op=ALU.add), r=[xT.s(mm), ps2[mm]], w=xT.s(mm))
                for _ in range(NI):
                    done_d()
                for m in range(NI, 8):
                    wd = get_d()
                    ps = nb()
                    for j in range(NJ):
                        k.pe(lambda e, j=j, ps=ps, wd=wd: e.matmul(ps[:], lhsT=wd[:, j, :], rhs=g[:, j, :], start=(j == 0), stop=(j == NJ - 1)),
                             r=[wd, g.s(j)], w=ps)
                    k.dve(lambda e, m=m, ps=ps: e.tensor_tensor(out=xT[:, m, :], in0=xT[:, m, :], in1=ps[:], op=ALU.add), r=[xT.s(m), ps], w=xT.s(m))
                    done_d()

            for c in range(8):
                xo = wk[c]
                if c % 2 == 0:
                    k.dve(lambda e, c=c, xo=xo: e.tensor_copy(out=xo[:], in_=xT[:, c, :]), r=xT.s(c), w=xo)
                else:
                    k.act(lambda e, c=c, xo=xo: e.activation(out=xo[:], in_=xT[:, c, :], func=AF.Copy), r=xT.s(c), w=xo)
                k.dma(lambda e, s_, c=c, xo=xo: e.dma_start(out=out_d[:, c, t * T:(t + 1) * T], in_=xo[:]).then_inc(s_, 16),
                      xo, r=xo, w=Dout.s(t * 8 + c))
    except _Stop:
        pass
    k.op("sp", lambda e: e.nop(), r=Dout)
    k.finalize()
    return nc, k


def _t5_bucket_np(dist):
    max_exact = 16
    d = np.maximum(dist, 0)
    d_f = np.maximum(d, 1).astype(np.float32)
    large = max_exact + (np.log(d_f / max_exact) / np.float32(np.log(128 / max_exact)) * (32 - max_exact)).astype(np.int32)
    large = np.minimum(large, 31)
    return np.where(d < max_exact, d, large)


def _pack_params(inp, L):
    f = np.float32
    P = np.zeros((L, 128, NPAR), f)
    for l in range(L):
        P[l, :, 0:8] = inp["attn_norm_g"][l].reshape(8, 128).T
        P[l, :, 8:16] = inp["ffn_norm_g"][l].reshape(8, 128).T
        P[l, :, 16:24] = inp["branch_scale"][l].reshape(8, 128).T
        P[l, :, 24] = inp["gla_b_a"][l]
        P[l, :, 25:27] = inp["gla_out_g"][l].reshape(2, 128).T
        P[l, :, 27] = np.tile(inp["swa_q_g"][l], 2)
        P[l, :, 28] = np.tile(inp["swa_k_g"][l], 2)
        P[l, :, 29:31] = inp["conv_dw_b"][l].reshape(2, 128).T
        P[l, :, 31:33] = inp["conv_ln_g"][l].reshape(2, 128).T
        P[l, :, 33:35] = inp["conv_ln_b"][l].reshape(2, 128).T
        P[l, :, 35:97] = inp["conv_dw_w"][l].reshape(31, 2, 128).transpose(2, 1, 0).reshape(128, 62)
        P[l, :, 97:141] = inp["ffn_conv_b"][l].reshape(44, 128).T
        P[l, :, 141:273] = inp["ffn_conv_w"][l].reshape(3, 44, 128).transpose(2, 1, 0).reshape(128, 132)
        P[l, :, 273:281] = np.tile(inp["swa_sinks"][l][None, :], (128, 1))
    return P


def _band_tables(rel_bias):
    j = np.arange(128)[:, None, None]
    pc = np.arange(2)[None, :, None]
    i = np.arange(128)[None, None, :]
    dist = (1 - pc) * 128 + i - j
    band = (dist >= 0) & (dist < 128)
    idx = _t5_bucket_np(dist)
    bt = np.asarray(rel_bias, np.float32)[idx]
    bt = bt.transpose(0, 3, 1, 2).reshape(128, 2, 4, 2, 128)
    bt = np.ascontiguousarray(bt.transpose(0, 1, 3, 2, 4)).reshape(128, 8 * 2 * 128)
    mk = np.ascontiguousarray(np.broadcast_to(band, (128, 2, 128))).astype(np.float32).reshape(128, 256)
    return bt, mk


_CACHE = {}


def _prep(inputs, L):
    f = np.float32
    inp = {k_: np.asarray(v) for k_, v in inputs.items()}
    bt, mk = _band_tables(inp["rel_bias"])
    common = {
        "w_in": np.ascontiguousarray(inp["w_in"][:L], f),
        "w_out": np.ascontiguousarray(inp["w_out"][:L], f),
        "w_up": np.ascontiguousarray(inp["w_up"][:L], f),
        "w_down": np.ascontiguousarray(inp["w_down"][:L], f),
        "wa2": np.ascontiguousarray(inp["gla_w_a2"][:L], f),
        "params": _pack_params(inp, L),
        "biasT": bt,
        "maskT": mk,
    }
    return inp, common


def kernel(**inputs):
    L = 2
    inp, common = _prep(inputs, L)
    x = np.asarray(inp["x"], np.float32)
    B, S, _ = x.shape
    key = (S, L)
    if key not in _CACHE:
        _CACHE[key] = build(S, L)[0]
    nc = _CACHE[key]
    in_maps = []
    for c in range(B):
        m = dict(common)
        m["x"] = np.ascontiguousarray(x[c].reshape(S, 8, 128).transpose(2, 1, 0))
        in_maps.append(m)
    res = run_bass_kernel_spmd(nc, in_maps, core_ids=list(range(B)))
    out = np.stack([np.asarray(r["out"], np.float32).transpose(2, 1, 0).reshape(S, D) for r in res.results], axis=0)
    return out
```
